# Optimizing a Trainium2 kernel written in Bass

```python
import math
import jax, jax.numpy as jnp
from jax import lax
import numpy as np

D_MODEL = 1024
BATCH = 8
SEQ = 2048
DEPTH = 2
DEC_BATCH = 128
DEC_SEQ = 4
PAST_LEN = 16384
PAGE_SIZE = 128

RET_HEADS = 4
RET_DK = D_MODEL // 8
RET_DV = D_MODEL // 4
RET_QK = RET_HEADS * RET_DK
RET_V = RET_HEADS * RET_DV
HG_HEADS = 8
HG_DK = D_MODEL // HG_HEADS
HG_DV = D_MODEL // HG_HEADS
HG_W = HG_HEADS * HG_DK
LRU_W = D_MODEL
LRU_BLOCKS = 8
LRU_BD = LRU_W // LRU_BLOCKS
CONV_W = 4
LRU_C = 8.0
N_BRANCH = 3
FFN_HIDDEN = 2816
CHUNK = 64
ROPE_BASE = 10000.0
EPS = 1e-6
IN_SPLITS = (RET_QK, RET_QK, RET_V, RET_V, HG_W, HG_W, HG_W, HG_W, LRU_W, LRU_W, N_BRANCH * D_MODEL)
IN_WIDTH = sum(IN_SPLITS)

kernel_name = "hybrid_retention_hgrn2_rglru_decode_step"


def _rmsnorm(x, g):
    xf = x.astype(jnp.float32)
    y = xf * lax.rsqrt(jnp.mean(xf * xf, axis=-1, keepdims=True) + EPS)
    return (y * g.astype(jnp.float32)).astype(x.dtype)


def _head_rms(x):
    return x * lax.rsqrt(jnp.mean(x * x, axis=-1, keepdims=True) + EPS)


def _swiglu(x, wg, wu, wd):
    return (jax.nn.silu(x @ wg) * (x @ wu)) @ wd


def _split(z):
    out = []
    o = 0
    for w in IN_SPLITS:
        out.append(z[..., o:o + w])
        o += w
    return out


def _rotary(x, pos):
    d = x.shape[-1]
    inv = ROPE_BASE ** (-jnp.arange(0, d, 2, dtype=jnp.float32) / d)
    ang = pos.astype(jnp.float32)[:, None] * inv[None, :]
    cos = jnp.cos(ang)[None, :, None, :]
    sin = jnp.sin(ang)[None, :, None, :]
    x1, x2 = x[..., : d // 2], x[..., d // 2:]
    return jnp.concatenate([x1 * cos - x2 * sin, x1 * sin + x2 * cos], axis=-1)


def _chunk_len(t):
    return CHUNK if t % CHUNK == 0 else t


def _to_chunks(x, c):
    b, t, h, d = x.shape
    return x.reshape(b, t // c, c, h, d).transpose(1, 0, 3, 2, 4)


def _from_chunks(y):
    n, b, h, c, d = y.shape
    return y.transpose(1, 0, 3, 2, 4).reshape(b, n * c, h, d)


def _retention(q, k, v, state):
    c = _chunk_len(q.shape[1])
    log_g = jnp.log1p(-jnp.exp2(-5.0 - jnp.arange(RET_HEADS, dtype=jnp.float32)))
    idx = jnp.arange(c, dtype=jnp.float32)
    diff = idx[:, None] - idx[None, :]
    causal = diff >= 0
    d_intra = jnp.where(causal, jnp.exp(log_g[:, None, None] * jnp.where(causal, diff, 0.0)), 0.0)
    q_dec = jnp.exp(log_g[:, None] * (idx + 1.0))[:, :, None]
    k_dec = jnp.exp(log_g[:, None] * (c - 1.0 - idx))[:, :, None]
    s_dec = jnp.exp(log_g * c)[:, None, None]

    def step(s, inp):
        qc, kc, vc = inp
        scores = jnp.einsum('bhtd,bhsd->bhts', qc, kc) * d_intra
        o = jnp.einsum('bhts,bhsv->bhtv', scores, vc) + jnp.einsum('bhtd,bhdv->bhtv', qc * q_dec, s)
        s = s * s_dec + jnp.einsum('bhsd,bhsv->bhdv', kc * k_dec, vc)
        return s, o

    s_fin, o = lax.scan(step, state, (_to_chunks(q, c), _to_chunks(k, c), _to_chunks(v, c)))
    return _from_chunks(o), s_fin


def _hgrn2(q, log_f, k, v, state):
    c = _chunk_len(q.shape[1])
    tri = jnp.tril(jnp.ones((c, c), dtype=bool))[:, :, None]

    def step(s, inp):
        qc, gc, kc, vc = inp
        b = jnp.cumsum(gc, axis=2)
        rel = jnp.where(tri, b[:, :, :, None, :] - b[:, :, None, :, :], -jnp.inf)
        a = jnp.einsum('bhtd,bhsd,bhtsd->bhts', qc, kc, jnp.exp(rel))
        o = jnp.einsum('bhts,bhsv->bhtv', a, vc) + jnp.einsum('bhtd,bhdv->bhtv', qc * jnp.exp(b), s)
        bl = b[:, :, -1:, :]
        s = s * jnp.exp(bl[:, :, 0, :, None]) + jnp.einsum('bhsd,bhsv->bhdv', kc * jnp.exp(bl - b), vc)
        return s, o

    s_fin, o = lax.scan(step, state, (_to_chunks(q, c), _to_chunks(log_f, c), _to_chunks(k, c), _to_chunks(v, c)))
    return _from_chunks(o), s_fin


def _causal_conv(x, buf, w, b):
    t = x.shape[1]
    xx = jnp.concatenate([buf, x], axis=1)
    y = b + sum(xx[:, i:i + t] * w[i] for i in range(CONV_W))
    return y, xx[:, xx.shape[1] - (CONV_W - 1):]


def _block_diag(x, w, b):
    bsz, t, _ = x.shape
    y = jnp.einsum('btnd,nde->btne', x.reshape(bsz, t, LRU_BLOCKS, LRU_BD), w)
    return y.reshape(bsz, t, LRU_W) + b


def _lin_combine(c1, c2):
    a1, b1 = c1
    a2, b2 = c2
    return a1 * a2, a2 * b1 + b2


def _rg_lru(x, h0, w_a, b_a, w_x, b_x, lam):
    r = jax.nn.sigmoid(_block_diag(x, w_a, b_a))
    i = jax.nn.sigmoid(_block_diag(x, w_x, b_x))
    log_a = -LRU_C * r * jax.nn.softplus(-lam)
    a = jnp.exp(log_a)
    u = jnp.sqrt(-jnp.expm1(2.0 * log_a)) * (i * x)
    u = u.at[:, 0].add(a[:, 0] * h0)
    _, hs = lax.associative_scan(_lin_combine, (a, u), axis=1)
    return hs, hs[:, -1]


def _mixer(h, pos, l, ret_s, hg_s, lru_h, conv_buf, p):
    f32 = jnp.float32
    bsz, t, _ = h.shape
    z = (h @ p['w_in'][l]).astype(f32)
    q_r, k_r, v_r, g_r, q_h, f_h, i_h, g_h, x_l, g_l, gate = _split(z)
    q = _rotary(q_r.reshape(bsz, t, RET_HEADS, RET_DK), pos)
    k = _rotary(k_r.reshape(bsz, t, RET_HEADS, RET_DK), pos) * (RET_DK ** -0.5)
    v = v_r.reshape(bsz, t, RET_HEADS, RET_DV)
    o_r, ret_new = _retention(q, k, v, ret_s.astype(f32))
    o_r = _head_rms(o_r).reshape(bsz, t, RET_V) * jax.nn.silu(g_r)
    br_r = (o_r.astype(h.dtype) @ p['w_ret_o'][l]).astype(f32)
    lbs = jnp.cumsum(jax.nn.softmax(p['hgrn_lb_logits'].astype(f32), axis=0), axis=0)
    lb = lbs[l] - lbs[0]
    log_f = jnp.logaddexp(jnp.log(lb), jnp.log1p(-lb) + jax.nn.log_sigmoid(f_h))
    k_h = -jnp.expm1(log_f)
    o_h, hg_new = _hgrn2(jax.nn.silu(q_h).reshape(bsz, t, HG_HEADS, HG_DK),
                         log_f.reshape(bsz, t, HG_HEADS, HG_DK),
                         k_h.reshape(bsz, t, HG_HEADS, HG_DK),
                         i_h.reshape(bsz, t, HG_HEADS, HG_DV), hg_s.astype(f32))
    o_h = _head_rms(o_h) * p['hgrn_norm'][l].astype(f32).reshape(HG_HEADS, HG_DV)
    o_h = o_h.reshape(bsz, t, HG_W) * jax.nn.silu(g_h)
    br_h = (o_h.astype(h.dtype) @ p['w_hgrn_o'][l]).astype(f32)
    xc, conv_new = _causal_conv(x_l, conv_buf.astype(f32), p['conv_w'][l].astype(f32), p['conv_b'][l].astype(f32))
    hs, h_new = _rg_lru(xc, lru_h.astype(f32), p['lru_w_a'][l].astype(f32), p['lru_b_a'][l].astype(f32),
                        p['lru_w_x'][l].astype(f32), p['lru_b_x'][l].astype(f32), p['lru_lambda'][l].astype(f32))
    br_l = ((hs * jax.nn.gelu(g_l)).astype(h.dtype) @ p['w_lru_o'][l]).astype(f32)
    gs = jax.nn.sigmoid(gate.reshape(bsz, t, N_BRANCH, D_MODEL) + p['merge_bias'][l].astype(f32))
    merged = gs[:, :, 0] * br_r + gs[:, :, 1] * br_h + gs[:, :, 2] * br_l
    out = merged.astype(h.dtype) @ p['w_mix_out'][l]
    return out, (ret_new, hg_new, h_new, conv_new)


def _trunk(x, pos, ret_s, hg_s, lru_h, conv_buf, p):
    news = ([], [], [], [])
    for l in range(DEPTH):
        x = x + 0.5 * _swiglu(_rmsnorm(x, p['ffn1_norm'][l]), p['ffn1_w_gate'][l], p['ffn1_w_up'][l], p['ffn1_w_down'][l])
        m, st = _mixer(_rmsnorm(x, p['mix_norm'][l]), pos, l, ret_s[l], hg_s[l], lru_h[l], conv_buf[l], p)
        x = x + m.astype(x.dtype)
        x = x + 0.5 * _swiglu(_rmsnorm(x, p['ffn2_norm'][l]), p['ffn2_w_gate'][l], p['ffn2_w_up'][l], p['ffn2_w_down'][l])
        for acc, s in zip(news, st):
            acc.append(s)
    y = _rmsnorm(x, p['final_norm'])
    return y, [jnp.stack(a, axis=0) for a in news]


def setup_inputs(seed: int = 0) -> dict:
    key = jax.random.key(seed)
    ks = iter(jax.random.split(key, 48))
    nrm = lambda shape, s: jax.random.normal(next(ks), shape, jnp.float32) * s
    gain = lambda shape: 1.0 + 0.01 * jax.random.normal(next(ks), shape, jnp.float32)
    d, f = D_MODEL, FFN_HIDDEN
    a0 = jax.random.uniform(next(ks), (DEPTH, LRU_W), jnp.float32, 0.9, 0.999)
    inp = {
        'x_prompt': nrm((BATCH, SEQ, d), 1.0),
        'x_sample': nrm((DEC_BATCH, DEC_SEQ, d), 1.0),
        'state_ret': nrm((DEPTH, DEC_BATCH, RET_HEADS, RET_DK, RET_DV), 1.0),
        'state_hgrn': nrm((DEPTH, DEC_BATCH, HG_HEADS, HG_DK, HG_DV), 1.0),
        'state_lru': nrm((DEPTH, DEC_BATCH, LRU_W), 0.5),
        'state_conv': nrm((DEPTH, DEC_BATCH, CONV_W - 1, LRU_W), 1.0),
        'ffn1_norm': gain((DEPTH, d)),
        'ffn1_w_gate': nrm((DEPTH, d, f), d ** -0.5),
        'ffn1_w_up': nrm((DEPTH, d, f), d ** -0.5),
        'ffn1_w_down': nrm((DEPTH, f, d), f ** -0.5),
        'mix_norm': gain((DEPTH, d)),
        'w_in': nrm((DEPTH, d, IN_WIDTH), d ** -0.5),
        'merge_bias': nrm((DEPTH, N_BRANCH, d), 0.01),
        'w_ret_o': nrm((DEPTH, RET_V, d), RET_V ** -0.5),
        'hgrn_lb_logits': nrm((DEPTH, HG_W), 0.5),
        'hgrn_norm': gain((DEPTH, HG_W)),
        'w_hgrn_o': nrm((DEPTH, HG_W, d), HG_W ** -0.5),
        'conv_w': nrm((DEPTH, CONV_W, LRU_W), CONV_W ** -0.5),
        'conv_b': nrm((DEPTH, LRU_W), 0.01),
        'lru_w_a': nrm((DEPTH, LRU_BLOCKS, LRU_BD, LRU_BD), LRU_BD ** -0.5),
        'lru_b_a': nrm((DEPTH, LRU_W), 0.01),
        'lru_w_x': nrm((DEPTH, LRU_BLOCKS, LRU_BD, LRU_BD), LRU_BD ** -0.5),
        'lru_b_x': nrm((DEPTH, LRU_W), 0.01),
        'lru_lambda': jnp.log(a0) - jnp.log1p(-a0),
        'w_lru_o': nrm((DEPTH, LRU_W, d), LRU_W ** -0.5),
        'w_mix_out': nrm((DEPTH, d, d), d ** -0.5),
        'ffn2_norm': gain((DEPTH, d)),
        'ffn2_w_gate': nrm((DEPTH, d, f), d ** -0.5),
        'ffn2_w_up': nrm((DEPTH, d, f), d ** -0.5),
        'ffn2_w_down': nrm((DEPTH, f, d), f ** -0.5),
        'final_norm': gain((d,)),
    }
    return inp


def reference(x_prompt, x_sample, state_ret, state_hgrn, state_lru, state_conv,
              ffn1_norm, ffn1_w_gate, ffn1_w_up, ffn1_w_down, mix_norm, w_in, merge_bias,
              w_ret_o, hgrn_lb_logits, hgrn_norm, w_hgrn_o, conv_w, conv_b,
              lru_w_a, lru_b_a, lru_w_x, lru_b_x, lru_lambda, w_lru_o, w_mix_out,
              ffn2_norm, ffn2_w_gate, ffn2_w_up, ffn2_w_down, final_norm):
    p = dict(ffn1_norm=ffn1_norm, ffn1_w_gate=ffn1_w_gate, ffn1_w_up=ffn1_w_up, ffn1_w_down=ffn1_w_down,
             mix_norm=mix_norm, w_in=w_in, merge_bias=merge_bias, w_ret_o=w_ret_o,
             hgrn_lb_logits=hgrn_lb_logits, hgrn_norm=hgrn_norm, w_hgrn_o=w_hgrn_o,
             conv_w=conv_w, conv_b=conv_b, lru_w_a=lru_w_a, lru_b_a=lru_b_a,
             lru_w_x=lru_w_x, lru_b_x=lru_b_x, lru_lambda=lru_lambda, w_lru_o=w_lru_o,
             w_mix_out=w_mix_out, ffn2_norm=ffn2_norm, ffn2_w_gate=ffn2_w_gate,
             ffn2_w_up=ffn2_w_up, ffn2_w_down=ffn2_w_down, final_norm=final_norm)
    f32 = jnp.float32
    bp = x_prompt.shape[0]
    pos_p = jnp.arange(x_prompt.shape[1])
    z_ret = jnp.zeros((DEPTH, bp, RET_HEADS, RET_DK, RET_DV), f32)
    z_hg = jnp.zeros((DEPTH, bp, HG_HEADS, HG_DK, HG_DV), f32)
    z_h = jnp.zeros((DEPTH, bp, LRU_W), f32)
    z_cv = jnp.zeros((DEPTH, bp, CONV_W - 1, LRU_W), f32)
    y_prompt, st_p = _trunk(x_prompt, pos_p, z_ret, z_hg, z_h, z_cv, p)
    pos_s = PAST_LEN + jnp.arange(x_sample.shape[1])
    y_sample, st_s = _trunk(x_sample, pos_s, state_ret, state_hgrn, state_lru, state_conv, p)
    dt = x_prompt.dtype
    return (y_prompt, y_sample,
            st_p[0].astype(dt), st_s[0].astype(state_ret.dtype),
            st_p[1].astype(dt), st_s[1].astype(state_hgrn.dtype),
            st_p[2].astype(dt), st_s[2].astype(state_lru.dtype),
            st_p[3].astype(dt), st_s[3].astype(state_conv.dtype))
```

```python
import os
import numpy as np
import concourse.bass as bass
import concourse.mybir as mybir
from concourse.bass_utils import run_bass_kernel_spmd
from contextlib import ExitStack

F32 = mybir.dt.float32
BF16 = mybir.dt.bfloat16
AF = mybir.ActivationFunctionType
ALU = mybir.AluOpType

D = 1024
SEQ = 2048
DEPTH = 2
NCORES = 8
PAST_LEN = 16384
FF = 2816
NP = 1024
NSQ = 8
NS = NSQ * 4
N = NP + NS
TT3 = [(0, 352), (352, 704), (704, 1056)]
TK = [(i * 128, 128) for i in range(8)] + [(1024, 32)]
EPS = 1e-6
KD = 8
NW = 2
NSCR = 8

VN = ['ffn1_norm', 'mix_norm', 'ffn2_norm', 'mb0', 'mb1', 'mb2', 'lb_logits', 'hgrn_norm',
      'cw0', 'cw1', 'cw2', 'cw3', 'conv_b', 'lru_b_a', 'lru_b_x', 'lru_lambda']
NVL = len(VN)
NV = 2 * NVL + 1


def VI(l, name):
    return l * NVL + VN.index(name)


def tts_of(c0, c1):
    return [i for i, (a, b) in enumerate(TT3) if a < c1 and c0 < b]


class Op:
    __slots__ = ('eng', 'fn', 'deps', 'idx', 'dma', 'sig', 'sem', 'val', 'ndep', 'raw')


class Sched:
    def __init__(self):
        self.ops = []
        self.last_w = {}
        self.readers = {}

    def add(self, eng, fn, reads=(), writes=(), dma=False):
        op = Op()
        op.eng = eng; op.fn = fn; op.dma = dma; op.idx = len(self.ops)
        op.sig = None; op.sem = None; op.val = None; op.ndep = 0
        deps = set()
        reads = [k for x in reads for k in (x if isinstance(x, list) else [x])]
        writes = [k for x in writes for k in (x if isinstance(x, list) else [x])]
        lw = self.last_w; rd = self.readers
        for k in reads:
            w = lw.get(k)
            if w is not None:
                deps.add(w)
        op.raw = set(deps)
        for k in writes:
            w = lw.get(k)
            if w is not None:
                deps.add(w)
            r = rd.get(k)
            if r:
                deps.update(r)
        for k in reads:
            rd.setdefault(k, []).append(op.idx)
        for k in writes:
            lw[k] = op.idx
            rd[k] = []
        deps.discard(op.idx)
        op.deps = deps
        self.ops.append(op)
        return op


def build_program(dbg=None):
    dbg = dbg or {}
    nc = bass.Bass("TRN2", target_bir_lowering=False)
    KD = dbg.get('kd', 4)
    S = Sched()
    marks = []

    def MARK(name):
        marks.append((name, sum(1 for o in S.ops if o.eng == 'pe')))
    es = ExitStack()

    def din(name, shape, dt=F32):
        return nc.dram_tensor(name, list(shape), dt, kind="ExternalInput").ap()

    def dout(name, shape, dt=F32):
        return nc.dram_tensor(name, list(shape), dt, kind="ExternalOutput").ap()

    def sb(name, shape, dt=F32):
        return es.enter_context(nc.sbuf_tensor(name, list(shape), dt))

    xin = din("xin", [2, N, D])
    sret = din("sret", [2, 16, 4, 128, 256])
    shg = din("shg", [2, 16, 8, 128, 128])
    slru = din("slru", [2, 16, D])
    sconv = din("sconv", [2, 16, 3, D])
    wg_d = [din("ffn1_w_gate", [2, D, FF]), din("ffn2_w_gate", [2, D, FF])]
    wu_d = [din("ffn1_w_up", [2, D, FF]), din("ffn2_w_up", [2, D, FF])]
    wd_d = [din("ffn1_w_down", [2, FF, D]), din("ffn2_w_down", [2, FF, D])]
    win_d = din("w_in", [2, D, 12288])
    wro_d = din("w_ret_o", [2, D, D])
    who_d = din("w_hgrn_o", [2, D, D])
    wlo_d = din("w_lru_o", [2, D, D])
    wmo_d = din("w_mix_out", [2, D, D])
    lwa_d = din("lru_w_a", [2, 8, 128, 128])
    lwx_d = din("lru_w_x", [2, 8, 128, 128])
    vecs_d = din("vecs", [128, NV, 8])
    cs_d = din("tab_cs", [2, 128, 9, 2, 64])
    dec_d = din("tab_dec", [128, 2, 3, 4])
    mk_d = din("tab_mask", [128, 64])
    mks_d = din("tab_masks", [32, 32])
    oh_d = din("tab_oh", [32, 8])
    id_d = din("tab_ident", [128, 128])

    y_o = dout("y", [2, N, D])
    retp_o = dout("retp", [2, 4, 128, 256])
    rets_o = dout("rets", [2, 16, 4, 128, 256])
    hgp_o = dout("hgp", [2, 8, 128, 128])
    hgs_o = dout("hgs", [2, 16, 8, 128, 128])
    lrup_o = dout("lrup", [2, D])
    lrus_o = dout("lrus", [2, 16, D])
    convp_o = dout("convp", [2, 3, D])
    convs_o = dout("convs", [2, 16, 3, D])

    xT = sb("xT", [128, 8, N])
    hT = sb("hT", [128, 8, N], BF16)
    ob = sb("ob", [128, 8, N], BF16)
    gt = sb("gt", [128, 8, N], BF16)
    wst = [sb("wst%d" % i, [128, 8, 256]) for i in range(NW)]
    wbf = [sb("wbf%d" % i, [128, 8, 256], BF16) for i in range(NW)]
    rstd = sb("rstd", [128, N])
    sqt = [sb("sqt%d" % i, [128, 352], BF16) for i in range(2)]
    tmpA = [sb("tmpA%d" % i, [128, 352]) for i in range(2)]
    Tbig = sb("Tbig", [128, NSCR, N])
    T = [Tbig[:, i, :] for i in range(NSCR)]
    XLp = T[7]
    vbig_r = T[7][0:32, :].bitcast(BF16)[:, 0:2048].rearrange("p (b v) -> p b v", b=8)
    vbig_h = T[0][0:32, :].bitcast(BF16)[:, 0:2048].rearrange("p (b v) -> p b v", b=8)
    QT = sb("QT", [128, 2, N], BF16)
    KT = sb("KT", [128, 2, N], BF16)
    K2 = sb("K2", [128, 9, 256], BF16)
    Vt = sb("Vt", [128, 9, 256], BF16)
    SR = [[sb("SR%d_%d" % (l, h), [128, 256]) for h in range(4)] for l in range(2)]
    SH = [[sb("SH%d_%d" % (l, h), [128, 128]) for h in range(8)] for l in range(2)]
    SIN = sb("SIN", [128, 1024])
    SOUT = sb("SOUT", [128, 1024])
    SR16S = sb("SR16S", [128, 1024], BF16)
    scm = [sb("scm%d" % i, [128, 64], BF16) for i in range(2)]
    scm_all = sb("scm_all", [128, 8, 64], BF16)
    SRalt = sb("SRalt", [128, 256])
    dummy = sb("fence_dummy", [128, 2])
    cs_t = sb("cs_t", [128, 9, 2, 64])
    dec_t = sb("dec_t", [128, 2, 3, 4])
    mk_t = sb("mk_t", [128, 64])
    mks_t = sb("mks_t", [32, 32])
    oh_t = sb("oh_t", [32, 8])
    id32 = sb("id32", [128, 128])
    id16 = sb("id16", [128, 128], BF16)
    ones16 = sb("ones16", [128, 128], BF16)
    ones_c = sb("ones_c", [128, 2])
    vecs = sb("vecs_t", [128, NV, 8])
    lb_t = sb("lb_t", [128, 2, 8])
    omlb_t = sb("omlb_t", [128, 2, 8])
    m8sp = sb("m8sp", [128, 2, 8])
    carry_h = sb("carry_h", [128, 2, 8])
    carry_c = sb("carry_c", [128, 2, 8, 3])
    XLs = sb("XLs", [128, NSQ, 7])
    XLs2 = sb("XLs2", [128, NSQ, 7])
    SCt = sb("SCt", [128, 8, 24])
    SLt = sb("SLt", [128, 8, 8])
    CN = sb("CN", [128, 8, 27])
    HN = sb("HN", [128, 8, 9])
    er2 = sb("er2", [128, 2, 24])
    eb2 = sb("eb2", [128, 2, 24])
    ebr2 = sb("ebr2", [128, 2, 24])
    hs_t = sb("hs_t", [128, NSQ])
    hs_t2 = sb("hs_t2", [128, NSQ])
    ytok = [T[0], T[1]]
    tok_a = Vt[:].rearrange("p a b -> p (a b)").bitcast(F32)
    tok_b = QT[:].rearrange("p a b -> p (a b)").bitcast(F32)
    TOKA = [('V', kk) for kk in range(9)]
    TOKB = [('QT', tt_) for tt_ in range(3)]

    ps = [es.enter_context(nc.psum_tensor("ps%d" % i, [128, 512], F32)) for i in range(8)]
    ps_ctr = [0]

    lps_ctr = [0]

    def bank():
        b = ps_ctr[0] % 6
        ps_ctr[0] += 1
        return b

    def lbank():
        b = 6 + lps_ctr[0] % 2
        lps_ctr[0] += 1
        return b

    def MM(out, lhsT, rhs, start, stop, r, w):
        S.add('pe', lambda e: e.matmul(out, lhsT=lhsT, rhs=rhs, start=start, stop=stop), r, w)

    def TR(out, in_, ident, r, w):
        S.add('pe', lambda e: e.transpose(out=out, in_=in_, identity=ident), r, w)

    def ACT(out, in_, func, r, w, bias=None, scale=None):
        kw = {}
        if bias is not None:
            kw['bias'] = bias
        if scale is not None:
            kw['scale'] = scale
        S.add('act', lambda e: e.activation(out=out, in_=in_, func=func, **kw), r, w)

    def TTo(eng, out, a, b, op, r, w):
        S.add(eng, lambda e: e.tensor_tensor(out=out, in0=a, in1=b, op=op), r, w)

    def TS(eng, out, a, s1, s2, op0, op1, r, w):
        if op1 is None and eng == 'pool':
            S.add(eng, lambda e: e.tensor_scalar(out=out, in0=a, scalar1=s1, scalar2=0.0, op0=op0, op1=ALU.add), r, w)
        elif op1 is None:
            S.add(eng, lambda e: e.tensor_scalar(out=out, in0=a, scalar1=s1, scalar2=None, op0=op0), r, w)
        else:
            S.add(eng, lambda e: e.tensor_scalar(out=out, in0=a, scalar1=s1, scalar2=s2, op0=op0, op1=op1), r, w)

    def STT(out, a, s, b, op0, op1, r, w):
        S.add('dve', lambda e: e.scalar_tensor_tensor(out=out, in0=a, scalar=s, in1=b, op0=op0, op1=op1), r, w)

    def CP(eng, out, in_, r, w):
        if eng == 'act':
            S.add('act', lambda e: e.copy(out=out, in_=in_), r, w)
        else:
            S.add(eng, lambda e: e.tensor_copy(out=out, in_=in_), r, w)

    def MSET(eng, ap, val, w):
        S.add(eng, lambda e: e.memset(ap, val), (), w)

    def DMA(out, in_, r, w, slow=False):
        if slow:
            S.add('sp', lambda e: e.dma_start(out=out, in_=in_, allow_slow_non_contiguous=True), r, w, dma=True)
        else:
            S.add('sp', lambda e: e.dma_start(out=out, in_=in_), r, w, dma=True)

    def FENCE(rkeys, wkeys):
        S.add('dve', lambda e: e.memset(dummy[:], 0.0), list(rkeys) + ['fence_dummy'], list(wkeys) + ['fence_dummy'])

    def kx(kc, tts):
        return [('x', kc, t) for t in tts]

    def kall(name, *idx):
        return [(name,) + tuple(idx) + (t,) for t in range(3)]

    wctr = [0]
    sctr = [0]
    ring_small = ([(wst[i], [('wst', i)]) for i in range(NW)], [(wbf[i], [('wbf', i)]) for i in range(NW)])
    Tflat = Tbig[:].rearrange("p a b -> p (a b)")
    stg_big = list(ring_small[0])
    for j in range(4):
        stg_big.append((Tflat[:, 2 * N * j:2 * N * j + 2048].rearrange("p (k c) -> p k c", k=8),
                        [('T', i_, t_) for i_ in (2 * j, 2 * j + 1) for t_ in range(3)]))
    bf_big = list(ring_small[1])
    bf_big.append((QT[:].rearrange("p a b -> p (a b)")[:, 0:2048].rearrange("p (k c) -> p k c", k=8), [('QT', t_) for t_ in range(3)]))
    bf_big.append((KT[:].rearrange("p a b -> p (a b)")[:, 0:2048].rearrange("p (k c) -> p k c", k=8), [('KT', t_) for t_ in range(3)]))
    bf_big.append((K2[:].rearrange("p a b -> p (a b)")[:, 0:2048].rearrange("p (k c) -> p k c", k=8), [('K2', k_) for k_ in range(9)]))
    bf_big.append((Vt[:].rearrange("p a b -> p (a b)")[:, 0:2048].rearrange("p (k c) -> p k c", k=8), [('V', k_) for k_ in range(9)]))
    ring_big = (stg_big, bf_big)
    ring = [ring_small]
    cast_pat = dbg.get('cast_pat', ['dve'])
    cast_big = dbg.get('cast_big', ['pool', 'act'])

    def wtile(src, kcn, cols=256):
        i = wctr[0]
        wctr[0] += 1
        stgs, bfs = ring[0]
        st_t, st_k = stgs[sctr[0] % len(stgs)]
        bf_t, bf_k = bfs[i % len(bfs)]
        sctr[0] += 1
        DMA(st_t[:, 0:kcn, 0:cols], src.rearrange("(kc p) n -> p kc n", p=128), (), [st_k])
        pat = cast_big if ring[0] is ring_big else cast_pat
        eng = pat[i % len(pat)]
        CP(eng, bf_t[:, 0:kcn, 0:cols], st_t[:, 0:kcn, 0:cols], [st_k], [bf_k])
        return bf_t, bf_k

    class WStream:
        def __init__(self, reqs, pd=dbg.get('pd', 3)):
            self.reqs = reqs; self.pd = pd; self.issued = []

        def get(self, i):
            while len(self.issued) < min(len(self.reqs), i + 1 + self.pd):
                src, kcn, cols = self.reqs[len(self.issued)]
                self.issued.append(wtile(src, kcn, cols))
            return self.issued[i]

    DMA(vecs[:], vecs_d, (), ['vecs'])
    DMA(cs_t[:], cs_d[0], (), ['cs'])
    DMA(dec_t[:], dec_d, (), ['dec'])
    DMA(mk_t[:], mk_d, (), ['mk'])
    DMA(mks_t[:], mks_d, (), ['mks'])
    DMA(oh_t[:], oh_d, (), ['oh'])
    DMA(id32[:], id_d, (), ['id32'])
    CP('dve', id16[:], id32[:], ['id32'], ['id16'])
    MSET('dve', ones16[:], 1.0, ['ones16'])
    MSET('pool', ones_c[:], 1.0, ['ones32'])
    MSET('dve', lb_t[:, 0, :], 0.0, ['lb'])
    TTo('dve', lb_t[:, 1, :], vecs[:, VI(1, 'lb_logits'), :], vecs[:, VI(0, 'lb_logits'), :], ALU.subtract, ['vecs', 'lb'], ['lb'])
    ACT(lb_t[:, 1, :], lb_t[:, 1, :], AF.Sigmoid, ['lb'], ['lb'])
    TS('dve', omlb_t[:], lb_t[:], -1.0, 1.0, ALU.mult, ALU.add, ['lb'], ['omlb'])
    for l in range(2):
        ACT(m8sp[:, l, :], vecs[:, VI(l, 'lru_lambda'), :], AF.Exp, ['vecs', 'm8sp'], ['m8sp'], scale=-1.0)
    ACT(m8sp[:], m8sp[:], AF.Ln, ['m8sp'], ['m8sp'], bias=1.0)
    TS('dve', m8sp[:], m8sp[:], -8.0, None, ALU.mult, None, ['m8sp'], ['m8sp'])
    for l in range(2):
        for h in range(4):
            MSET('pool', SR[l][h][:], 0.0, [('SR', l, h)])
        for h in range(8):
            MSET('pool', SH[l][h][:], 0.0, [('SH', l, h)])
    MSET('dve', carry_h[:], 0.0, ['carry_h'])
    MSET('dve', carry_c[:], 0.0, ['carry_c'])

    def emit_norm(vidx):
        for ti, (c0, c1) in enumerate(TT3):
            b = bank()
            for kc in range(8):
                sq = sqt[kc % 2]
                ACT(sq[:, 0:c1 - c0], xT[:, kc, c0:c1], AF.Square, [('x', kc, ti)], [('sqt', kc % 2)])
                MM(ps[b][:, 0:c1 - c0], ones16[:], sq[:, 0:c1 - c0], kc == 0, kc == 7,
                   [('sqt', kc % 2), 'ones16'], [('ps', b)])
            ACT(rstd[:, c0:c1], ps[b][:, 0:c1 - c0], AF.Ln, [('ps', b)], [('rs', ti)], bias=EPS, scale=1.0 / D)
            ACT(rstd[:, c0:c1], rstd[:, c0:c1], AF.Exp, [('rs', ti)], [('rs', ti)], scale=-0.5)
            for kc in range(8):
                STT(hT[:, kc, c0:c1], xT[:, kc, c0:c1], vecs[:, vidx, kc:kc + 1], rstd[:, c0:c1],
                    ALU.mult, ALU.mult, [('x', kc, ti), ('rs', ti), 'vecs'], [('h', kc, ti)])

    def fm_proj(wt, wkey, mloc, src, skey, kcn, consume):
        for ti, (c0, c1) in enumerate(TT3):
            b = bank()
            for kc in range(kcn):
                MM(ps[b][:, 0:c1 - c0], wt[:, kc, mloc * 128:(mloc + 1) * 128], src[:, kc, c0:c1],
                   kc == 0, kc == kcn - 1, [wkey, (skey, kc, ti)], [('ps', b)])
            consume(ti, c0, c1, b)

    def emit_ffn(l, which):
        wg, wu, wd = wg_d[which][l], wu_d[which][l], wd_d[which][l]
        ring[0] = ring_big
        groups = [(0, 6), (6, 12), (12, 17), (17, 22)]
        reqs = []
        for (j0, j1) in groups:
            j = j0
            while j < j1:
                nj = min(2, j1 - j)
                reqs.append((wg[:, j * 128:(j + nj) * 128], 8, nj * 128))
                reqs.append((wu[:, j * 128:(j + nj) * 128], 8, nj * 128))
                j += nj
            for mp in range(4):
                reqs.append((wd[j0 * 128:j1 * 128, mp * 256:(mp + 1) * 256], j1 - j0, 256))
        ws = WStream(reqs)
        wi = [0]

        def nxt():
            r = ws.get(wi[0])
            wi[0] += 1
            return r
        for (j0, j1) in groups[:dbg.get('ffn_groups', 4)]:
            j = j0
            while j < j1:
                nj = min(2, j1 - j)
                wgt, wgk = nxt()
                wut, wuk = nxt()
                for m in range(nj):
                    jj = j + m - j0
                    for ti, (c0, c1) in enumerate(TT3):
                        w_ = c1 - c0
                        bg = bank()
                        for kc in range(8):
                            MM(ps[bg][:, 0:w_], wgt[:, kc, m * 128:(m + 1) * 128], hT[:, kc, c0:c1], kc == 0, kc == 7,
                               [wgk, ('h', kc, ti)], [('ps', bg)])
                        bu = bank()
                        for kc in range(8):
                            MM(ps[bu][:, 0:w_], wut[:, kc, m * 128:(m + 1) * 128], hT[:, kc, c0:c1], kc == 0, kc == 7,
                               [wuk, ('h', kc, ti)], [('ps', bu)])
                        ta = tmpA[ti % 2]
                        ACT(ta[:, 0:w_], ps[bg][:, 0:w_], AF.Silu, [('ps', bg)], [('tmpA', ti % 2)])
                        TTo('dve', gt[:, jj, c0:c1], ps[bu][:, 0:w_], ta[:, 0:w_], ALU.mult,
                            [('ps', bu), ('tmpA', ti % 2)], [('g', jj, ti)])
                j += nj
            ng = j1 - j0
            for mp in range(4 if dbg.get('ffn_b', 1) else 0):
                wdt, wdk = nxt()
                for m in range(2):
                    mi = mp * 2 + m
                    for ti, (c0, c1) in enumerate(TT3):
                        w_ = c1 - c0
                        b = bank()
                        for jj in range(ng):
                            MM(ps[b][:, 0:w_], wdt[:, jj, m * 128:(m + 1) * 128], gt[:, jj, c0:c1], jj == 0, jj == ng - 1,
                               [wdk, ('g', jj, ti)], [('ps', b)])
                        STT(xT[:, mi, c0:c1], ps[b][:, 0:w_], 0.5, xT[:, mi, c0:c1], ALU.mult, ALU.add,
                            [('ps', b), ('x', mi, ti)], [('x', mi, ti)])
        ring[0] = ring_small

    def head_norm_cols(srcs, nfeat, ti, c0, c1):
        w_ = c1 - c0
        b = bank()
        for i, (apf, key) in enumerate(srcs):
            sq = sqt[i % 2]
            ACT(sq[:, 0:w_], apf(c0, c1), AF.Square, [key + (ti,)], [('sqt', i % 2)])
            MM(ps[b][:, 0:w_], ones16[:], sq[:, 0:w_], i == 0, i == len(srcs) - 1, [('sqt', i % 2), 'ones16'], [('ps', b)])
        ACT(rstd[:, c0:c1], ps[b][:, 0:w_], AF.Ln, [('ps', b)], [('rs', ti)], bias=EPS, scale=1.0 / nfeat)
        ACT(rstd[:, c0:c1], rstd[:, c0:c1], AF.Exp, [('rs', ti)], [('rs', ti)], scale=-0.5)

    def emit_branch_out(l, bi, wo_d):
        ring[0] = ring_big
        reqs = []
        for mp in range(4):
            reqs.append((wo_d[l][:, mp * 256:(mp + 1) * 256], 8, 256))
            reqs.append((win_d[l][:, 9216 + bi * 1024 + mp * 256: 9216 + bi * 1024 + (mp + 1) * 256], 8, 256))
        for mp in range(4):
            reqs.append((wmo_d[l][:, mp * 256:(mp + 1) * 256], 8, 256))
        ws = WStream(reqs)
        for mp in range(4):
            wot, wok = ws.get(2 * mp)
            wgt, wgk = ws.get(2 * mp + 1)
            for m in range(2):
                mi = mp * 2 + m
                for ti, (c0, c1) in enumerate(TT3):
                    w_ = c1 - c0
                    bb = bank()
                    for kc in range(8):
                        MM(ps[bb][:, 0:w_], wot[:, kc, m * 128:(m + 1) * 128], ob[:, kc, c0:c1], kc == 0, kc == 7,
                           [wok, ('o', kc, ti)], [('ps', bb)])
                    bg = bank()
                    for kc in range(8):
                        MM(ps[bg][:, 0:w_], wgt[:, kc, m * 128:(m + 1) * 128], hT[:, kc, c0:c1], kc == 0, kc == 7,
                           [wgk, ('h', kc, ti)], [('ps', bg)])
                    ta = tmpA[ti % 2]
                    ACT(ta[:, 0:w_], ps[bg][:, 0:w_], AF.Sigmoid, [('ps', bg), 'vecs'], [('tmpA', ti % 2)],
                        bias=vecs[:, VI(l, 'mb%d' % bi), mi:mi + 1])
                    TTo('dve', gt[:, mi, c0:c1], ps[bb][:, 0:w_], ta[:, 0:w_], ALU.mult,
                        [('ps', bb), ('tmpA', ti % 2)], [('g', mi, ti)])
        for mp in range(4):
            wmt, wmk = ws.get(8 + mp)
            for m in range(2):
                mi = mp * 2 + m
                for ti, (c0, c1) in enumerate(TT3):
                    w_ = c1 - c0
                    b = bank()
                    for kc in range(8):
                        MM(ps[b][:, 0:w_], wmt[:, kc, m * 128:(m + 1) * 128], gt[:, kc, c0:c1], kc == 0, kc == 7,
                           [wmk, ('g', kc, ti)], [('ps', b)])
                    TTo('dve', xT[:, mi, c0:c1], ps[b][:, 0:w_], xT[:, mi, c0:c1], ALU.add,
                        [('ps', b), ('x', mi, ti)], [('x', mi, ti)])
        ring[0] = ring_small

    def emit_lru(l, p):
        for wi, wdram in enumerate((lwa_d, lwx_d)):
            st_t, st_k = ring[0][0][sctr[0] % len(ring[0][0])]
            sctr[0] += 1
            DMA(st_t[:, 0:8, 0:128], wdram[l].rearrange("n d e -> d n e"), (), [st_k])
            CP('dve', K2[:, 0:8, wi * 128:(wi + 1) * 128], st_t[:, 0:8, 0:128], [st_k], [('K2', kk) for kk in range(9)])
        DMA(tok_a[0:24, 0:D], sconv[l, p * 8:(p + 1) * 8].rearrange("b j w -> (b j) w"), (), TOKA)
        for half in range(2):
            b = bank()
            for q in range(4):
                n = half * 4 + q
                TR(ps[b][:, q * 32:q * 32 + 24], tok_a[0:24, n * 128:(n + 1) * 128], id32[0:24, 0:24], TOKA + ['id32'], [('ps', b)])
            CP('dve', SCt[:, half * 4:(half + 1) * 4, :], ps[b][:, 0:128].rearrange("p (q r) -> p q r", r=32)[:, :, 0:24],
               [('ps', b)], ['SCt'])
        DMA(tok_a[0:8, 0:D], slru[l, p * 8:(p + 1) * 8], TOKA, TOKA)
        for half in range(2):
            b = bank()
            for q in range(4):
                n = half * 4 + q
                TR(ps[b][:, q * 32:q * 32 + 8], tok_a[0:8, n * 128:(n + 1) * 128], id32[0:8, 0:8], TOKA + ['id32'], [('ps', b)])
            CP('dve', SLt[:, half * 4:(half + 1) * 4, :], ps[b][:, 0:128].rearrange("p (q r) -> p q r", r=32)[:, :, 0:8],
               [('ps', b)], ['SLt'])

        G32 = gt[:].rearrange("p a b -> p (a b)").bitcast(F32)

        def gk(i):
            return lambda t=None: [('g', 2 * i + j_, t_) for j_ in range(2) for t_ in (range(3) if t is None else [t])]

        def tk(i):
            return lambda t=None: [('T', i, t_) for t_ in (range(3) if t is None else [t])]

        def nk(name, cnt=3):
            return lambda t=None: [(name, t_) for t_ in (range(cnt) if t is None else [t])]
        B0 = dict(XLp=T[7], kXLp=tk(7), XC=T[0], kXC=tk(0), XCb=T[1][:].bitcast(BF16), kXCb=tk(1), R=T[2], kR=tk(2),
                  I=T[3], kI=tk(3), A=T[4], kA=tk(4), HS=T[6], kHS=tk(6), XLs=XLs, kXLs='XLs', hs=hs_t, khs='hs_t')
        B1 = dict(XLp=G32[:, 0:N], kXLp=gk(0), XC=G32[:, N:2 * N], kXC=gk(1),
                  XCb=Vt[:].rearrange("p a b -> p (a b)"), kXCb=(lambda t=None: [('V', k_) for k_ in range(9)]),
                  R=G32[:, 2 * N:3 * N], kR=gk(2), I=G32[:, 3 * N:4 * N], kI=gk(3),
                  A=QT[:].rearrange("p a b -> p (a b)").bitcast(F32), kA=nk('QT'),
                  HS=KT[:].rearrange("p a b -> p (a b)").bitcast(F32), kHS=nk('KT'), XLs=XLs2, kXLs='XLs2', hs=hs_t2, khs='hs_t2')

        def chunk_gen(n, m, wxt, wxk, wgt, wgk, B):
            XLp_, XC, XCb16, R, I, A, HS, XLs_, hs_ = B['XLp'], B['XC'], B['XCb'], B['R'], B['I'], B['A'], B['HS'], B['XLs'], B['hs']
            kXLp, kXC, kXCb, kR, kI, kA, kHS, kXLs, khs = B['kXLp'], B['kXC'], B['kXCb'], B['kR'], B['kI'], B['kA'], B['kHS'], B['kXLs'], B['khs']
            CP('dve', XLp_[:, 0:3], carry_c[:, l, n, :], ['carry_c'] + kXLp(), kXLp())
            CP('dve', XLs_[:, :, 0:3], SCt[:, n, :].rearrange("p (b j) -> p b j", j=3), ['SCt', kXLs], [kXLs])

            def cons_x(ti, c0, c1, b):
                pe = min(c1, NP)
                if pe > c0:
                    CP('act', XLp_[:, 3 + c0:3 + pe], ps[b][:, 0:pe - c0], [('ps', b)] + kXLp(), kXLp())
                if c1 > NP:
                    CP('act', XLs_[:, :, 3:7], ps[b][:, NP - c0:c1 - c0].rearrange("p (b t) -> p b t", t=4),
                       [('ps', b), kXLs], [kXLs])
            fm_proj(wxt, wxk, m, hT, 'h', 8, cons_x)
            yield
            cw = [vecs[:, VI(l, 'cw%d' % i), n:n + 1] for i in range(4)]
            cb = vecs[:, VI(l, 'conv_b'), n:n + 1]
            xcp = XC[:, 0:NP]
            xcs = XC[:, NP:N].rearrange("p (b t) -> p b t", t=4)
            TS('dve', xcp, XLp_[:, 3:3 + NP], cw[3], cb, ALU.mult, ALU.add, kXLp() + ['vecs'], kXC())
            TS('dve', xcs, XLs_[:, :, 3:7], cw[3], cb, ALU.mult, ALU.add, [kXLs, 'vecs'], kXC())
            for i in (2, 1, 0):
                STT(xcp, XLp_[:, i:i + NP], cw[i], xcp, ALU.mult, ALU.add, kXLp() + ['vecs'] + kXC(), kXC())
                STT(xcs, XLs_[:, :, i:i + 4], cw[i], xcs, ALU.mult, ALU.add, [kXLs, 'vecs'] + kXC(), kXC())
            yield
            CP('pool', CN[:, n, 0:24].rearrange("p (b j) -> p b j", j=3), XLs_[:, :, 4:7], [kXLs, 'CN'], ['CN'])
            CP('pool', CN[:, n, 24:27], XLp_[:, NP:NP + 3], kXLp() + ['CN'], ['CN'])
            CP('pool', carry_c[:, l, n, :], XLp_[:, NP:NP + 3], kXLp() + ['carry_c'], ['carry_c'])
            CP('act', XCb16[:, 0:N], XC[:, :], kXC(), kXCb())
            yield
            for gi, (dst, kd, bname) in enumerate(((R, kR, 'lru_b_a'), (I, kI, 'lru_b_x'))):
                for ti, (c0, c1) in enumerate(TT3):
                    b = bank()
                    MM(ps[b][:, 0:c1 - c0], K2[:, n, gi * 128:(gi + 1) * 128], XCb16[:, c0:c1], True, True, [('K2', n)] + kXCb(ti), [('ps', b)])
                    ACT(dst[:, c0:c1], ps[b][:, 0:c1 - c0], AF.Sigmoid, [('ps', b), 'vecs'], kd(ti),
                        bias=vecs[:, VI(l, bname), n:n + 1])
            yield
            ACT(A[:, :], R[:, :], AF.Exp, kR() + ['m8sp'], kA(), scale=m8sp[:, l, n:n + 1])
            ACT(R[:, :], A[:, :], AF.Square, kA(), kR())
            ACT(R[:, :], R[:, :], AF.Sqrt, kR(), kR(), bias=1.0, scale=-1.0)
            yield
            TTo('dve', I[:, :], I[:, :], XC[:, :], ALU.mult, kI() + kXC(), kI())
            TTo('dve', I[:, :], I[:, :], R[:, :], ALU.mult, kI() + kR(), kI())
            U = I
            S.add('dve', lambda e, n=n: e.tensor_tensor_scan(out=HS[:, 0:NP], data0=A[:, 0:NP], data1=U[:, 0:NP],
                                                             initial=carry_h[:, l, n:n + 1], op0=ALU.mult, op1=ALU.add),
                  kA() + kI() + ['carry_h'], kHS())
            yield
            a_s = A[:, NP:N].rearrange("p (b t) -> p b t", t=4)
            u_s = U[:, NP:N].rearrange("p (b t) -> p b t", t=4)
            h_s = HS[:, NP:N].rearrange("p (b t) -> p b t", t=4)
            for t in range(4):
                prev = SLt[:, n, :] if t == 0 else h_s[:, :, t - 1]
                TTo('dve', hs_[:, :], a_s[:, :, t], prev, ALU.mult, kA() + kHS() + ['SLt', khs], [khs])
                TTo('dve', h_s[:, :, t], hs_[:, :], u_s[:, :, t], ALU.add, [khs] + kI() + kHS(), kHS())
            CP('pool', HN[:, n, 0:8], h_s[:, :, 3], kHS() + ['HN'], ['HN'])
            CP('pool', HN[:, n, 8:9], HS[:, NP - 1:NP], kHS() + ['HN'], ['HN'])
            CP('pool', carry_h[:, l, n:n + 1], HS[:, NP - 1:NP], kHS() + ['carry_h'], ['carry_h'])
            yield

            def cons_g(ti, c0, c1, b):
                ta = tmpA[ti % 2]
                ACT(ta[:, 0:c1 - c0], ps[b][:, 0:c1 - c0], AF.Gelu, [('ps', b)], [('tmpA', ti % 2)])
                TTo('dve', ob[:, n, c0:c1], HS[:, c0:c1], ta[:, 0:c1 - c0], ALU.mult,
                    kHS(ti) + [('tmpA', ti % 2)], [('o', n, ti)])
            fm_proj(wgt, wgk, m, hT, 'h', 8, cons_g)
            yield
        for t2 in range(4):
            wxt, wxk = wtile(win_d[l][:, 7168 + t2 * 256:7168 + (t2 + 1) * 256], 8)
            wgt, wgk = wtile(win_d[l][:, 8192 + t2 * 256:8192 + (t2 + 1) * 256], 8)
            gens = [chunk_gen(t2 * 2, 0, wxt, wxk, wgt, wgk, B0), chunk_gen(t2 * 2 + 1, 1, wxt, wxk, wgt, wgk, B1)]
            live = list(gens)
            while live:
                for g_ in list(live):
                    try:
                        next(g_)
                    except StopIteration:
                        live.remove(g_)
        for half in range(2):
            b = bank()
            for q in range(4):
                n = half * 4 + q
                TR(ps[b][0:27, q * 128:(q + 1) * 128], CN[:, n, :], id32[:, :], ['CN', 'id32'], [('ps', b)])
            CP('dve', tok_b[0:27, half * 512:(half + 1) * 512], ps[b][0:27, :], [('ps', b)] + TOKB, TOKB)
        DMA(convs_o[l, p * 8:(p + 1) * 8].rearrange("b j w -> (b j) w"), tok_b[0:24, 0:D], TOKB, [('out', 'convs', l, p)] + TOKB)
        if p == 1:
            DMA(convp_o[l], tok_b[24:27, 0:D], TOKB, [('out', 'convp', l)] + TOKB)
        for half in range(2):
            b = bank()
            for q in range(4):
                n = half * 4 + q
                TR(ps[b][0:9, q * 128:(q + 1) * 128], HN[:, n, :], id32[:, :], ['HN', 'id32'], [('ps', b)])
            CP('dve', tok_b[0:9, half * 512:(half + 1) * 512], ps[b][0:9, :], [('ps', b)] + TOKB, TOKB)
        DMA(lrus_o[l, p * 8:(p + 1) * 8], tok_b[0:8, 0:D], TOKB, [('out', 'lrus', l, p)] + TOKB)
        if p == 1:
            DMA(lrup_o[l:l + 1, :], tok_b[8:9, 0:D], TOKB, [('out', 'lrup', l)] + TOKB)
        MARK('p%d l%d lru_out' % (p, l))
        emit_branch_out(l, 2, wlo_d)

    def tm_proj(wt, wkey, ncols, k, consume_bank):
        c0, R = TK[k]
        b = bank()
        tts = tts_of(c0, c0 + R)
        for kc in range(8):
            MM(ps[b][0:R, 0:ncols], hT[:, kc, c0:c0 + R], wt[:, kc, 0:ncols], kc == 0, kc == 7,
               [wkey] + [('h', kc, t) for t in tts], [('ps', b)])
        consume_bank(k, R, b)

    def emit_ret(l, p):
        gam = [1.0 - 2.0 ** (-5 - h) for h in range(4)]
        t1, t2, t3, t4, rot = T[0], T[1], T[2], T[3], T[4]
        OT = [T[5], T[6]]
        for hp in range(2):
            wqt, wqk = wtile(win_d[l][:, hp * 256:(hp + 1) * 256], 8)
            for which in range(2):
                if which == 1:
                    wqt, wqk = wtile(win_d[l][:, 512 + hp * 256:512 + (hp + 1) * 256], 8)

                pending = []

                def cons_qk(k, R, b, which=which):
                    while pending:
                        pending.pop(0)()
                    kind = 0 if k < 8 else 1
                    pv = ps[b][0:R, 0:256].rearrange("p (h two j) -> p h two j", h=2, two=2)
                    x1 = pv[:, :, 0, :]
                    x2 = pv[:, :, 1, :]
                    cosb = cs_t[0:R, k, 0, :].unsqueeze(1).broadcast_to([R, 2, 64])
                    sinb = cs_t[0:R, k, 1, :].unsqueeze(1).broadcast_to([R, 2, 64])
                    v = lambda t_: t_[0:R, 0:128].rearrange("p (h j) -> p h j", h=2)
                    rk = [('ps', b), 'cs']
                    TTo('dve', v(t1), x1, cosb, ALU.mult, rk, [('T', 0, 0)])
                    TTo('dve', v(t2), x2, sinb, ALU.mult, rk, [('T', 1, 0)])
                    TTo('dve', v(t3), x1, sinb, ALU.mult, rk, [('T', 2, 0)])
                    TTo('dve', v(t4), x2, cosb, ALU.mult, rk, [('T', 3, 0)])
                    rv = rot[0:R, 0:256].rearrange("p (h two j) -> p h two j", h=2, two=2)
                    TTo('dve', rv[:, :, 0, :], v(t1), v(t2), ALU.subtract, [('T', 0, 0), ('T', 1, 0)], [('T', 4, 0)])
                    TTo('dve', rv[:, :, 1, :], v(t3), v(t4), ALU.add, [('T', 2, 0), ('T', 3, 0)], [('T', 4, 0)])
                    rv2 = rot[0:R, 0:256].rearrange("p (h d) -> p h d", h=2)
                    tq = (t1 if k % 2 == 0 else t2)[0:R, 256:512].bitcast(BF16)[:, 0:256]
                    ktq = ('tq', k % 2)
                    tqv = tq.rearrange("p (h d) -> p h d", h=2)
                    di = 0 if which == 0 else 1
                    decb = dec_t[0:R, kind, di, hp * 2:hp * 2 + 2].unsqueeze(2).broadcast_to([R, 2, 128])
                    TTo('dve', tqv, rv2, decb, ALU.mult, [('T', 4, 0), 'dec'], [ktq])
                    if which == 1:
                        decb2 = dec_t[0:R, kind, 2, hp * 2:hp * 2 + 2].unsqueeze(2).broadcast_to([R, 2, 128])
                        TTo('pool', K2[0:R, k, :].rearrange("p (h d) -> p h d", h=2), rv2, decb2, ALU.mult,
                            [('T', 4, 0), 'dec'], [('K2', k)])
                    def part_b(k=k, R=R, tq=tq, ktq=ktq, which=which):
                        bt = bank()
                        pbt = ps[bt][:, 0:128].bitcast(BF16)
                        for hh in range(2):
                            TR(pbt[:, hh * 128:hh * 128 + R], tq[:, hh * 128:(hh + 1) * 128], id16[0:R, 0:R], [ktq, 'id16'], [('ps', bt)])
                        dst = QT if which == 0 else KT
                        c0 = TK[k][0]
                        CP('act', dst[:, :, c0:c0 + R], pbt.rearrange("p (h r) -> p h r", h=2)[:, :, 0:R],
                           [('ps', bt)], [('QT' if which == 0 else 'KT', t) for t in tts_of(c0, c0 + R)])
                    pending.append(part_b)
                for k in range(9):
                    tm_proj(wqt, wqk, 256, k, cons_qk)
                while pending:
                    pending.pop(0)()
            for hh in range(2):
                h = hp * 2 + hh
                wvt, wvk = wtile(win_d[l][:, 1024 + h * 256:1024 + (h + 1) * 256], 8)

                def cons_v(k, R, b):
                    CP('act', Vt[0:R, k, :], ps[b][0:R, 0:256], [('ps', b)], [('V', k)])
                for k in range(9):
                    tm_proj(wvt, wvk, 256, k, cons_v)
                gC = gam[h] ** 64
                for pr in range(8):
                    bs = bank()
                    for e_ in range(2):
                        cc0 = (pr * 2 + e_) * 64
                        tts = tts_of(cc0, cc0 + 64)
                        MM(ps[bs][e_ * 64:e_ * 64 + 64, 0:64], KT[:, hh, cc0:cc0 + 64], QT[:, hh, cc0:cc0 + 64], True, True,
                           [('KT', t) for t in tts] + [('QT', t) for t in tts], [('ps', bs)])
                    TTo('dve', scm_all[:, pr, :], ps[bs][:, 0:64], mk_t[:, :], ALU.mult, [('ps', bs), 'mk'], [('scm_all', pr)])

                def sh16(c):
                    return T[c // 8][:, :].bitcast(BF16)[:, (c % 8) * 256:(c % 8) * 256 + 256]
                cur, alt = SR[l][h], SRalt
                kcur, kalt = ('SR', l, h), ('SRalt',)
                CP('act', sh16(0), cur[:], [kcur], [('sh16', 0)] + kall('T', 0))
                for c in range(16):
                    k = c // 2
                    base = (c % 2) * 64
                    bk = bank()
                    MM(ps[bk][:, 0:256], K2[base:base + 64, k, hh * 128:(hh + 1) * 128], Vt[base:base + 64, k, :], True, True,
                       [('K2', k), ('V', k)], [('ps', bk)])
                    (src, ksrc), (dst, kdst) = ((cur, kcur), (alt, kalt)) if c % 2 == 0 else ((alt, kalt), (cur, kcur))
                    STT(dst[:], src[:], gC, ps[bk][:, 0:256], ALU.mult, ALU.add, [ksrc, ('ps', bk)], [kdst])
                    if c < 15:
                        CP('act', sh16(c + 1), dst[:], [kdst], [('sh16', c + 1)] + [('T', (c + 1) // 8, t) for t in range(3)])
                ob_bank = None
                for c in range(16):
                    k = c // 2
                    base = (c % 2) * 64
                    cc0 = c * 64
                    tts = tts_of(cc0, cc0 + 64)
                    if c % 4 == 0:
                        ob_bank = lbank()
                    for vc in range(2):
                        oc = vc * 256 + (c % 4) * 64
                        MM(ps[ob_bank][:, oc:oc + 64], Vt[base:base + 64, k, vc * 128:(vc + 1) * 128], scm_all[base:base + 64, c // 2, :],
                           True, False, [('V', k), ('scm_all', c // 2)], [('ps', ob_bank)])
                        MM(ps[ob_bank][:, oc:oc + 64], sh16(c)[:, vc * 128:(vc + 1) * 128], QT[:, hh, cc0:cc0 + 64],
                           False, True, [('sh16', c)] + [('T', c // 8, t) for t in range(3)] + [('QT', t) for t in tts], [('ps', ob_bank)])
                    if c % 4 == 3:
                        g0 = (c - 3) * 64
                        for vc in range(2):
                            CP('act', OT[vc][:, g0:g0 + 256], ps[ob_bank][:, vc * 256:(vc + 1) * 256], [('ps', ob_bank)],
                               [('T', 5 + vc, t) for t in tts_of(g0, g0 + 256)])
                bs = bank()
                MM(ps[bs][0:32, 0:32], KT[:, hh, NP:N], QT[:, hh, NP:N], True, True, [('KT', 2), ('QT', 2)], [('ps', bs)])
                TTo('dve', scm[0][0:32, 0:32], ps[bs][0:32, 0:32], mks_t[:, :], ALU.mult, [('ps', bs), 'mks'], [('scm', 0)])
                TTo('pool', vbig_r[:, :, :], Vt[0:32, 8, :].unsqueeze(1).broadcast_to([32, 8, 256]),
                    oh_t[:, :].unsqueeze(2).broadcast_to([32, 8, 256]), ALU.mult, [('V', 8), 'oh'], kall('T', 7))
                bo2 = [lbank(), lbank()]
                for vc in range(2):
                    MM(ps[bo2[vc]][:, 0:32], Vt[0:32, 8, vc * 128:(vc + 1) * 128], scm[0][0:32, 0:32], True, False,
                       [('V', 8), ('scm', 0)], [('ps', bo2[vc])])
                g4 = gam[h] ** 4
                SINr = SIN[:, :].rearrange("p (b v) -> p b v", b=4)
                SOUTr = SOUT[:, :].rearrange("p (b v) -> p b v", b=4)
                S16r = SR16S[:, :].rearrange("p (b v) -> p b v", b=4)
                for g_ in range(2):
                    b0 = p * 8 + g_ * 4
                    DMA(SINr, sret[l, b0:b0 + 4, h].rearrange("b d v -> d b v"), (), ['SIN'])
                    CP('pool', S16r, SINr, ['SIN'], ['SR16S'])
                    for j_ in range(4):
                        bq = g_ * 4 + j_
                        for vc in range(2):
                            MM(ps[bo2[vc]][:, bq * 4:bq * 4 + 4], S16r[:, j_, vc * 128:(vc + 1) * 128],
                               QT[:, hh, NP + bq * 4:NP + bq * 4 + 4], False, (bq == NSQ - 1), ['SR16S', ('QT', 2)], [('ps', bo2[vc])])
                        bk = bank()
                        MM(ps[bk][:, 0:256], K2[0:32, 8, hh * 128:(hh + 1) * 128], vbig_r[:, bq, :], True, True, [('K2', 8)] + kall('T', 7), [('ps', bk)])
                        STT(SOUTr[:, j_, :], SINr[:, j_, :], g4, ps[bk][:, 0:256], ALU.mult, ALU.add, ['SIN', ('ps', bk)], [('SOUT', j_)])
                    DMA(rets_o[l, b0:b0 + 4, h].rearrange("b d v -> d b v"), SOUTr, [('SOUT', j_) for j_ in range(4)],
                        [('out', 'rets', l, b0, h)] + [('SOUT', j_) for j_ in range(4)])
                for vc in range(2):
                    CP('act', OT[vc][:, NP:N], ps[bo2[vc]][:, 0:32], [('ps', bo2[vc])], [('T', 5 + vc, 2)])
                if p == 1:
                    DMA(retp_o[l, h], SR[l][h][:], [('SR', l, h)], [('out', 'retp', l, h)])
                wgt, wgk = wtile(win_d[l][:, 2048 + h * 256:2048 + (h + 1) * 256], 8)
                for ti, (c0, c1) in enumerate(TT3):
                    head_norm_cols([(lambda a, b_, vc=vc: OT[vc][:, a:b_], ('T', 5 + vc)) for vc in range(2)], 256, ti, c0, c1)
                for vc in range(2):
                    def cons_g(ti, c0, c1, b, vc=vc):
                        w_ = c1 - c0
                        ta = tmpA[ti % 2]
                        ACT(ta[:, 0:w_], ps[b][:, 0:w_], AF.Silu, [('ps', b)], [('tmpA', ti % 2)])
                        TTo('dve', ta[:, 0:w_], ta[:, 0:w_], rstd[:, c0:c1], ALU.mult, [('tmpA', ti % 2), ('rs', ti)], [('tmpA', ti % 2)])
                        TTo('dve', ob[:, h * 2 + vc, c0:c1], OT[vc][:, c0:c1], ta[:, 0:w_], ALU.mult,
                            [('T', 5 + vc, ti), ('tmpA', ti % 2)], [('o', h * 2 + vc, ti)])
                    fm_proj(wgt, wgk, vc, hT, 'h', 8, cons_g)
        MARK('p%d l%d ret_out' % (p, l))
        emit_branch_out(l, 0, wro_d)

    def emit_hgrn(l, p):
        Qf, Ff, LF, Kf, Bc, E1, E2, OTh = T[0], T[1], T[2], T[3], T[4], T[5], T[6], T[7]
        G32 = gt[:].rearrange("p a b -> p (a b)").bitcast(F32)
        G16 = gt[:].rearrange("p a b -> p (a b)")
        gkeys = [('g', kc_, t_) for kc_ in range(8) for t_ in range(3)]
        fine = [('sh32', c_) for c_ in range(17)] + [('sr16', c_) for c_ in range(16)] + [('xs', c_) for c_ in range(4)] + [('vbh',)]
        FENCE(gkeys, gkeys + fine)

        def sh32(c):
            return G32[:, c * 128:(c + 1) * 128]

        def xs(c):
            return G32[:, 2048 + (c % 4) * 128:2048 + (c % 4) * 128 + 128]

        def sr16(c):
            return G16[:, 5120 + c * 128:5120 + (c + 1) * 128]
        vbh = G16[0:32, 7168:8192].rearrange("p (b v) -> p b v", b=8)

        def prep(h):
            sl_ = h % 2
            er_, eb_, ebr_ = er2[:, sl_, :], eb2[:, sl_, :], ebr2[:, sl_, :]
            ker, keb, kebr = ('er', sl_), ('eb', sl_), ('ebr', sl_)
            kq = [('QT%d' % sl_, t) for t in range(3)]
            kk = [('KT%d' % sl_, t) for t in range(3)]
            wq_, wqk = wtile(win_d[l][:, 3072 + h * 128:3072 + (h + 1) * 128], 8, 128)
            wf_, wfk = wtile(win_d[l][:, 4096 + h * 128:4096 + (h + 1) * 128], 8, 128)

            def cons_q(ti, c0, c1, b):
                ACT(Qf[:, c0:c1], ps[b][:, 0:c1 - c0], AF.Silu, [('ps', b)], [('T', 0, ti)])
            fm_proj(wq_, wqk, 0, hT, 'h', 8, cons_q)

            def cons_f(ti, c0, c1, b):
                ACT(Ff[:, c0:c1], ps[b][:, 0:c1 - c0], AF.Sigmoid, [('ps', b)], [('T', 1, ti)])
            fm_proj(wf_, wfk, 0, hT, 'h', 8, cons_f)
            TS('dve', Ff[:, :], Ff[:, :], omlb_t[:, l, h:h + 1], lb_t[:, l, h:h + 1], ALU.mult, ALU.add,
               kall('T', 1) + ['lb', 'omlb'], kall('T', 1))
            ACT(LF[:, :], Ff[:, :], AF.Ln, kall('T', 1), kall('T', 2))
            ACT(Kf[:, :], Ff[:, :], AF.Identity, kall('T', 1), kall('T', 3), bias=1.0, scale=-1.0)
            S.add('dve', lambda e: e.tensor_tensor_scan(out=Bc[:, :], data0=ones_c[:, 0:1].broadcast_to([128, N]), data1=LF[:, :], initial=0.0,
                                                        op0=ALU.mult, op1=ALU.add),
                  kall('T', 2) + ['ones32'], kall('T', 4))
            Bp = Bc[:, 0:NP].rearrange("p (c t) -> p c t", t=64)
            TTo('dve', LF[:, 0:NP].rearrange("p (c t) -> p c t", t=64), Bp, Bp[:, :, 32:33].broadcast_to([128, 16, 64]),
                ALU.subtract, kall('T', 4), kall('T', 2))
            TTo('dve', LF[:, NP:N], Bc[:, NP:N], Bc[:, NP + 16:NP + 17].broadcast_to([128, 32]), ALU.subtract,
                kall('T', 4) + kall('T', 2), kall('T', 2))
            ACT(E1[:, :], LF[:, :], AF.Exp, kall('T', 2), kall('T', 5))
            ACT(E2[:, :], LF[:, :], AF.Exp, kall('T', 2), kall('T', 6), scale=-1.0)
            TTo('dve', QT[:, sl_, :], Qf[:, :], E1[:, :], ALU.mult, kall('T', 0) + kall('T', 5), kq + [('QT', t) for t in range(3)])
            TTo('dve', KT[:, sl_, :], Kf[:, :], E2[:, :], ALU.mult, kall('T', 3) + kall('T', 6), kk + [('KT', t) for t in range(3)])
            e2p = E2[:, 0:NP].rearrange("p (c t) -> p c t", t=64)[:, :, 0]
            ffp = Ff[:, 0:NP].rearrange("p (c t) -> p c t", t=64)[:, :, 0]
            e1p = E1[:, 0:NP].rearrange("p (c t) -> p c t", t=64)[:, :, 63]
            e2s = E2[:, NP:N].rearrange("p (b t) -> p b t", t=4)[:, :, 0]
            ffs = Ff[:, NP:N].rearrange("p (b t) -> p b t", t=4)[:, :, 0]
            e1s = E1[:, NP:N].rearrange("p (b t) -> p b t", t=4)[:, :, 3]
            TTo('dve', er_[:, 0:16], e2p, ffp, ALU.mult, kall('T', 6) + kall('T', 1) + [ker], [ker])
            TTo('dve', er_[:, 16:24], e2s, ffs, ALU.mult, kall('T', 6) + kall('T', 1) + [ker], [ker])
            CP('dve', ebr_[:, 0:16], e1p, kall('T', 5) + [kebr], [kebr])
            CP('dve', ebr_[:, 16:24], e1s, kall('T', 5) + [kebr], [kebr])
            TTo('dve', eb_[:, :], ebr_[:, :], er_[:, :], ALU.mult, [ker, kebr, keb], [keb])

        def recur(h):
            hp, hh = h // 2, h % 2
            sl_ = h % 2
            er_t, eb_t, ebr_t = er2[:, sl_, :], eb2[:, sl_, :], ebr2[:, sl_, :]
            ker, keb, kebr = ('er', sl_), ('eb', sl_), ('ebr', sl_)
            kq = [('QT%d' % sl_, t) for t in range(3)]
            kk = [('KT%d' % sl_, t) for t in range(3)]
            ko = sl_ * 128
            DMA(SIN[:, :].rearrange("p (b v) -> p b v", b=8), shg[l, p * 8:(p + 1) * 8, h].rearrange("b d v -> d b v"), (), ['SIN'])
            for k in range(9):
                c0, R = TK[k]
                bt = bank()
                pbt = ps[bt][:, 0:64].bitcast(BF16)
                TR(pbt[0:R, 0:128], KT[:, sl_, c0:c0 + R], id16[:, :], kk + ['id16'], [('ps', bt)])
                CP('act', K2[0:R, k, ko:ko + 128], pbt[0:R, 0:128], [('ps', bt)], [('K2h%d' % sl_, k), ('K2', k)])
            if hh == 0:
                wi_, wik = wtile(win_d[l][:, 5120 + hp * 256:5120 + (hp + 1) * 256], 8)

                def cons_v(k, R, b):
                    CP('act', Vt[0:R, k, :], ps[b][0:R, 0:256], [('ps', b)], [('V', k)])
                for k in range(9):
                    tm_proj(wi_, wik, 256, k, cons_v)
            for pr in range(8):
                bs = bank()
                for e_ in range(2):
                    cc0 = (pr * 2 + e_) * 64
                    MM(ps[bs][e_ * 64:e_ * 64 + 64, 0:64], KT[:, sl_, cc0:cc0 + 64], QT[:, sl_, cc0:cc0 + 64], True, True,
                       kk + kq, [('ps', bs)])
                TTo('dve', scm_all[:, pr, :], ps[bs][:, 0:64], mk_t[:, :], ALU.mult, [('ps', bs), 'mk'], [('scm_all', pr)])
            for c in range(16):
                k = c // 2
                base = (c % 2) * 64
                bk = bank()
                MM(ps[bk][:, 0:128], K2[base:base + 64, k, ko:ko + 128], Vt[base:base + 64, k, hh * 128:(hh + 1) * 128], True, True,
                   [('K2h%d' % sl_, k), ('V', k)], [('ps', bk)])
                ACT(xs(c), ps[bk][:, 0:128], AF.Identity, [('ps', bk), kebr], [('xs', c % 4)], scale=ebr_t[:, c:c + 1])
                prev, kprev = (SH[l][h][:], [('SH', l, h)]) if c == 0 else (sh32(c), [('sh32', c)])
                dst, kdst = (SH[l][h][:], [('SH', l, h)]) if c == 15 else (sh32(c + 1), [('sh32', c + 1)])
                TS('pool', sr16(c), prev, er_t[:, c:c + 1], None, ALU.mult, None, kprev + [ker], [('sr16', c)])
                STT(dst, prev, eb_t[:, c:c + 1], xs(c), ALU.mult, ALU.add, kprev + [keb, ('xs', c % 4)], kdst)
            ob_bank = None
            for c in range(16):
                k = c // 2
                base = (c % 2) * 64
                cc0 = c * 64
                if c % 8 == 0:
                    ob_bank = lbank()
                oc = (c % 8) * 64
                MM(ps[ob_bank][:, oc:oc + 64], Vt[base:base + 64, k, hh * 128:(hh + 1) * 128], scm_all[base:base + 64, c // 2, :],
                   True, False, [('V', k), ('scm_all', c // 2)], [('ps', ob_bank)])
                MM(ps[ob_bank][:, oc:oc + 64], sr16(c), QT[:, sl_, cc0:cc0 + 64], False, True,
                   [('sr16', c)] + kq, [('ps', ob_bank)])
                if c % 8 == 7:
                    g0 = (c - 7) * 64
                    CP('act', OTh[:, g0:g0 + 512], ps[ob_bank][:, 0:512], [('ps', ob_bank)], [('T', 7, t) for t in tts_of(g0, g0 + 512)])
            bs = bank()
            MM(ps[bs][0:32, 0:32], KT[:, sl_, NP:N], QT[:, sl_, NP:N], True, True, kk + kq, [('ps', bs)])
            TTo('dve', scm[0][0:32, 0:32], ps[bs][0:32, 0:32], mks_t[:, :], ALU.mult, [('ps', bs), 'mks'], [('scm', 0)])
            TTo('pool', vbh[:, :, :], Vt[0:32, 8, hh * 128:(hh + 1) * 128].unsqueeze(1).broadcast_to([32, 8, 128]),
                oh_t[:, :].unsqueeze(2).broadcast_to([32, 8, 128]), ALU.mult, [('V', 8), 'oh'], [('vbh',)])
            bo = lbank()
            MM(ps[bo][:, 0:32], Vt[0:32, 8, hh * 128:(hh + 1) * 128], scm[0][0:32, 0:32], True, False, [('V', 8), ('scm', 0)], [('ps', bo)])
            SINh = SIN[:, :].rearrange("p (b v) -> p b v", b=8)
            SOUTh = SOUT[:, :].rearrange("p (b v) -> p b v", b=8)
            S16h = SR16S[:, :].rearrange("p (b v) -> p b v", b=8)
            TTo('pool', S16h, SINh, er_t[:, 16:24].unsqueeze(2).broadcast_to([128, 8, 128]), ALU.mult, ['SIN', ker], ['SR16S'])
            TTo('pool', SINh, SINh, eb_t[:, 16:24].unsqueeze(2).broadcast_to([128, 8, 128]), ALU.mult, ['SIN', keb], ['SIN'])
            for bq in range(NSQ):
                MM(ps[bo][:, bq * 4:bq * 4 + 4], S16h[:, bq, :], QT[:, sl_, NP + bq * 4:NP + bq * 4 + 4], False, (bq == NSQ - 1),
                   ['SR16S'] + kq, [('ps', bo)])
                bk = bank()
                MM(ps[bk][:, 0:128], K2[0:32, 8, ko:ko + 128], vbh[:, bq, :], True, True, [('K2h%d' % sl_, 8), ('vbh',)], [('ps', bk)])
                STT(SOUTh[:, bq, :], ps[bk][:, 0:128], ebr_t[:, 16 + bq:17 + bq], SINh[:, bq, :], ALU.mult, ALU.add,
                    [('ps', bk), kebr, 'SIN'], [('SOUT', bq)])
            DMA(hgs_o[l, p * 8:(p + 1) * 8, h].rearrange("b d v -> d b v"), SOUTh, [('SOUT', b_) for b_ in range(8)],
                [('out', 'hgs', l, p, h)] + [('SOUT', b_) for b_ in range(8)])
            CP('act', OTh[:, NP:N], ps[bo][:, 0:32], [('ps', bo)], [('T', 7, 2)])
            if p == 1:
                DMA(hgp_o[l, h], SH[l][h][:], [('SH', l, h)], [('out', 'hgp', l, h)])
            wg_, wgk_ = wtile(win_d[l][:, 6144 + h * 128:6144 + (h + 1) * 128], 8, 128)
            for ti, (c0, c1) in enumerate(TT3):
                head_norm_cols([(lambda a, b_: OTh[:, a:b_], ('T', 7))], 128, ti, c0, c1)

            def cons_g(ti, c0, c1, b, h=h):
                w_ = c1 - c0
                ta = tmpA[ti % 2]
                ACT(ta[:, 0:w_], ps[b][:, 0:w_], AF.Silu, [('ps', b)], [('tmpA', ti % 2)])
                STT(ta[:, 0:w_], ta[:, 0:w_], vecs[:, VI(l, 'hgrn_norm'), h:h + 1], rstd[:, c0:c1], ALU.mult, ALU.mult,
                    [('tmpA', ti % 2), ('rs', ti), 'vecs'], [('tmpA', ti % 2)])
                TTo('dve', ob[:, h, c0:c1], OTh[:, c0:c1], ta[:, 0:w_], ALU.mult,
                    [('T', 7, ti), ('tmpA', ti % 2)], [('o', h, ti)])
            fm_proj(wg_, wgk_, 0, hT, 'h', 8, cons_g)

        slotk = [('QT%d' % s_, t) for s_ in range(2) for t in range(3)] + [('KT%d' % s_, t) for s_ in range(2) for t in range(3)] + \
                [('K2h%d' % s_, k_) for s_ in range(2) for k_ in range(9)]
        wholek = [('QT', t) for t in range(3)] + [('KT', t) for t in range(3)] + [('K2', k_) for k_ in range(9)]
        FENCE(wholek, wholek + slotk)
        prep(0)
        for h in range(8):
            if h + 1 < 8:
                prep(h + 1)
            recur(h)
        FENCE(slotk + fine, wholek + gkeys)
        MARK('p%d l%d hgrn_out' % (p, l))
        emit_branch_out(l, 1, who_d)

    which_layers = dbg.get('layers', 2)
    branches = dbg.get('branches', 'lrh')
    for p in range(2):
        if p == 1:
            DMA(cs_t[:], cs_d[1], ['cs'], ['cs'])
        for k, (c0, R) in enumerate(TK):
            yt = ytok[k % 2]
            DMA(yt[0:R, 0:D], xin[p, c0:c0 + R, :], (), kall('T', k % 2))
            for half in range(2):
                b = bank()
                for q in range(4):
                    kc = half * 4 + q
                    TR(ps[b][:, q * 128:q * 128 + R], yt[0:R, kc * 128:(kc + 1) * 128], id32[0:R, 0:R], kall('T', k % 2) + ['id32'], [('ps', b)])
                CP('dve' if half == 0 else 'act', xT[:, half * 4:(half + 1) * 4, c0:c0 + R],
                   ps[b][:, :].rearrange("p (q r) -> p q r", q=4)[:, :, 0:R], [('ps', b)],
                   [('x', kc, t) for kc in range(half * 4, half * 4 + 4) for t in tts_of(c0, c0 + R)])
        for l in range(which_layers):
            MARK('p%d l%d ffn1' % (p, l))
            emit_norm(VI(l, 'ffn1_norm'))
            emit_ffn(l, 0)
            MARK('p%d l%d mixnorm' % (p, l))
            emit_norm(VI(l, 'mix_norm'))
            if 'l' in branches:
                MARK('p%d l%d lru' % (p, l))
                emit_lru(l, p)
            if 'r' in branches:
                MARK('p%d l%d ret' % (p, l))
                emit_ret(l, p)
            if 'h' in branches:
                MARK('p%d l%d hgrn' % (p, l))
                emit_hgrn(l, p)
            MARK('p%d l%d ffn2' % (p, l))
            emit_norm(VI(l, 'ffn2_norm'))
            emit_ffn(l, 1)
        MARK('p%d final' % p)
        for ti, (c0, c1) in enumerate(TT3):
            b = bank()
            for kc in range(8):
                sq = sqt[kc % 2]
                ACT(sq[:, 0:c1 - c0], xT[:, kc, c0:c1], AF.Square, [('x', kc, ti)], [('sqt', kc % 2)])
                MM(ps[b][:, 0:c1 - c0], ones16[:], sq[:, 0:c1 - c0], kc == 0, kc == 7, [('sqt', kc % 2), 'ones16'], [('ps', b)])
            ACT(rstd[:, c0:c1], ps[b][:, 0:c1 - c0], AF.Ln, [('ps', b)], [('rs', ti)], bias=EPS, scale=1.0 / D)
            ACT(rstd[:, c0:c1], rstd[:, c0:c1], AF.Exp, [('rs', ti)], [('rs', ti)], scale=-0.5)
            for kc in range(8):
                STT(xT[:, kc, c0:c1], xT[:, kc, c0:c1], vecs[:, 2 * NVL, kc:kc + 1], rstd[:, c0:c1], ALU.mult, ALU.mult,
                    [('x', kc, ti), ('rs', ti), 'vecs'], [('x', kc, ti)])
        for k, (c0, R) in enumerate(TK):
            yt = ytok[k % 2]
            for half in range(2):
                b = bank()
                for q in range(4):
                    kc = half * 4 + q
                    TR(ps[b][0:R, q * 128:(q + 1) * 128], xT[:, kc, c0:c0 + R], id32[:, :],
                       [('x', kc, t) for t in tts_of(c0, c0 + R)] + ['id32'], [('ps', b)])
                CP('dve' if half == 0 else 'act', yt[0:R, half * 512:(half + 1) * 512], ps[b][0:R, :], [('ps', b)] + kall('T', k % 2), kall('T', k % 2))
            DMA(y_o[p, c0:c0 + R, :], yt[0:R, 0:D], kall('T', k % 2), [('out', 'y', p, k)] + kall('T', k % 2))

    ops = S.ops
    same_engine_sync = dbg.get('same_engine_sync', True)
    relax_war = dbg.get('relax_war', True)
    dma_ops = [o for o in ops if o.dma]
    for i, o in enumerate(dma_ops):
        if i >= KD:
            o.deps.add(dma_ops[i - KD].idx)
    for o in ops:
        nd = set()
        for d in o.deps:
            po = ops[d]
            if po.eng == 'pe' and o.eng == 'pe':
                continue
            if (not same_engine_sync) and po.eng == o.eng and not po.dma:
                continue
            if relax_war and po.eng == o.eng and not po.dma and not o.dma and d not in o.raw:
                continue
            nd.add(d)
        o.deps = nd
        for d in nd:
            ops[d].ndep += 1
    eng_names = ['pe', 'act', 'dve', 'pool', 'sp']
    sems = {e: es.enter_context(nc.semaphore("s_" + e)) for e in eng_names}
    dsems = [es.enter_context(nc.semaphore("d%d" % i)) for i in range(KD)]
    cnt = {e: 0 for e in eng_names}
    for i, o in enumerate(dma_ops):
        o.sem = dsems[i % KD]
        o.val = 16 * (i // KD + 1)
    for o in ops:
        if o.dma:
            continue
        if o.ndep > 0:
            cnt[o.eng] += 1
            o.sem = sems[o.eng]
            o.val = cnt[o.eng]
    per_eng = {e: [o for o in ops if o.eng == e] for e in eng_names}
    stats = {e: len(per_eng[e]) for e in eng_names}
    stats['marks'] = marks

    def emit_engine(e, ename):
        waited = {}
        for o in per_eng[ename]:
            need = {}
            for d in o.deps:
                po = ops[d]
                key = id(po.sem)
                if po.val > need.get(key, (None, 0))[1]:
                    need[key] = (po.sem, po.val)
            for key, (sem, val) in need.items():
                if waited.get(key, 0) >= val:
                    continue
                e.wait_ge(sem, val)
                waited[key] = val
            ins = o.fn(e)
            if o.dma:
                ins.then_inc(o.sem, 16)
            elif o.sem is not None:
                ins.then_inc(o.sem, 1)
        if ename == 'sp':
            nd = len(dma_ops)
            for j in range(KD):
                n_j = (nd - j + KD - 1) // KD if nd > j else 0
                if n_j > 0:
                    e.wait_ge(dsems[j], 16 * n_j)

    with nc.Block() as block:
        @block.tensor
        def _(e):
            emit_engine(e, 'pe')

        @block.scalar
        def _(e):
            emit_engine(e, 'act')

        @block.vector
        def _(e):
            emit_engine(e, 'dve')

        @block.gpsimd
        def _(e):
            emit_engine(e, 'pool')

        @block.sync
        def _(e):
            emit_engine(e, 'sp')
    es.close()
    return nc, stats


def _tables():
    inv = (np.float32(10000.0) ** (-(np.arange(0, 128, 2, dtype=np.float32)) / np.float32(128))).astype(np.float32)
    cs = np.zeros((2, 128, 9, 2, 64), np.float32)
    for p in range(2):
        for k in range(9):
            if k < 8:
                pos = (p * 1024 + k * 128 + np.arange(128)).astype(np.float32)
            else:
                pos = (PAST_LEN + (np.arange(128) % 4)).astype(np.float32)
            ang = (pos[:, None] * inv[None, :]).astype(np.float32).astype(np.float64)
            cs[p, :, k, 0, :] = np.cos(ang)
            cs[p, :, k, 1, :] = np.sin(ang)
    g = np.array([1.0 - 2.0 ** (-5 - h) for h in range(4)], np.float64)
    dec = np.zeros((128, 2, 3, 4), np.float64)
    sc = 128.0 ** -0.5
    for kind, C in ((0, 64), (1, 4)):
        tl = (np.arange(128) % C).astype(np.float64)[:, None]
        dec[:, kind, 0, :] = g[None, :] ** (tl + 1.0)
        dec[:, kind, 1, :] = g[None, :] ** (-(tl + 1.0)) * sc
        dec[:, kind, 2, :] = g[None, :] ** (C - 1.0 - tl) * sc
    s = np.arange(128)[:, None] % 64
    t = np.arange(64)[None, :]
    mk = (s <= t).astype(np.float32)
    s = np.arange(32)[:, None]
    t = np.arange(32)[None, :]
    mks = ((s // 4 == t // 4) & (s <= t)).astype(np.float32)
    oh = (np.arange(32)[:, None] // 4 == np.arange(8)[None, :]).astype(np.float32)
    ident = np.eye(128, dtype=np.float32)
    return cs, dec.astype(np.float32), mk, mks, oh, ident


_CACHE = {}


def kernel(**inp):
    f = lambda a: np.ascontiguousarray(np.asarray(a, dtype=np.float32))
    dbg = {}
    if os.environ.get('KDBG_LAYERS'):
        dbg['layers'] = int(os.environ['KDBG_LAYERS'])
    if os.environ.get('KDBG_BRANCHES') is not None:
        dbg['branches'] = os.environ['KDBG_BRANCHES']
    if os.environ.get('KDBG_FFNG'):
        dbg['ffn_groups'] = int(os.environ['KDBG_FFNG'])
    if os.environ.get('KDBG_FFNB'):
        dbg['ffn_b'] = int(os.environ['KDBG_FFNB'])
    if os.environ.get('KDBG_CAST'):
        dbg['cast_pat'] = tuple(os.environ['KDBG_CAST'].split(','))
    if os.environ.get('KDBG_KD'):
        dbg['kd'] = int(os.environ['KDBG_KD'])
    if os.environ.get('KDBG_PD'):
        dbg['pd'] = int(os.environ['KDBG_PD'])
    if os.environ.get('KDBG_CASTBIG'):
        dbg['cast_big'] = tuple(os.environ['KDBG_CASTBIG'].split(','))
    if os.environ.get('KDBG_RELAX'):
        dbg['relax_war'] = True
    if os.environ.get('KDBG_NOSES'):
        dbg['same_engine_sync'] = False
    key = tuple(sorted(dbg.items()))
    if key not in _CACHE:
        _CACHE[key] = build_program(dbg)
    nc, stats = _CACHE[key]
    cs, dec, mk, mks, oh, ident = _tables()
    vl = []
    for l in range(2):
        d = {'ffn1_norm': inp['ffn1_norm'][l], 'mix_norm': inp['mix_norm'][l], 'ffn2_norm': inp['ffn2_norm'][l],
             'mb0': inp['merge_bias'][l, 0], 'mb1': inp['merge_bias'][l, 1], 'mb2': inp['merge_bias'][l, 2],
             'lb_logits': inp['hgrn_lb_logits'][l], 'hgrn_norm': inp['hgrn_norm'][l],
             'cw0': inp['conv_w'][l, 0], 'cw1': inp['conv_w'][l, 1], 'cw2': inp['conv_w'][l, 2], 'cw3': inp['conv_w'][l, 3],
             'conv_b': inp['conv_b'][l], 'lru_b_a': inp['lru_b_a'][l], 'lru_b_x': inp['lru_b_x'][l], 'lru_lambda': inp['lru_lambda'][l]}
        for n_ in VN:
            vl.append(np.asarray(d[n_], np.float32))
    vl.append(np.asarray(inp['final_norm'], np.float32))
    vecs = np.ascontiguousarray(np.stack(vl, 0).reshape(NV, 8, 128).transpose(2, 0, 1))
    shared = {k_: f(inp[k_]) for k_ in ['ffn1_w_gate', 'ffn2_w_gate', 'ffn1_w_up', 'ffn2_w_up', 'ffn1_w_down', 'ffn2_w_down',
                                        'w_in', 'w_ret_o', 'w_hgrn_o', 'w_lru_o', 'w_mix_out', 'lru_w_a', 'lru_w_x']}
    shared.update(vecs=vecs, tab_cs=cs, tab_dec=dec, tab_mask=mk, tab_masks=mks, tab_oh=oh, tab_ident=ident)
    xp = f(inp['x_prompt']); xs = f(inp['x_sample'])
    in_maps = []
    for c in range(NCORES):
        xin = np.empty((2, N, D), np.float32)
        for p in range(2):
            xin[p, :NP] = xp[c, p * NP:(p + 1) * NP]
            xin[p, NP:] = xs[c * 16 + p * 8:c * 16 + (p + 1) * 8].reshape(NS, D)
        m = dict(shared)
        m['xin'] = xin
        sl = slice(c * 16, (c + 1) * 16)
        m['sret'] = f(inp['state_ret'][:, sl])
        m['shg'] = f(inp['state_hgrn'][:, sl])
        m['slru'] = f(inp['state_lru'][:, sl])
        m['sconv'] = f(inp['state_conv'][:, sl])
        in_maps.append(m)
    ncr = int(os.environ.get('KDBG_CORES', NCORES))
    if os.environ.get('KDBG_TRACE'):
        res = run_bass_kernel_spmd(nc, in_maps[:ncr], core_ids=list(range(ncr)), trace=True)
        print("EXEC_NS", res.exec_time_ns)
    else:
        res = run_bass_kernel_spmd(nc, in_maps[:ncr], core_ids=list(range(ncr)))
    R = list(res.results)
    while len(R) < NCORES:
        R.append(R[0])
    y_p = np.empty((8, SEQ, D), np.float32)
    y_s = np.empty((128, 4, D), np.float32)
    for c in range(NCORES):
        y = R[c]['y']
        for p in range(2):
            y_p[c, p * NP:(p + 1) * NP] = y[p, :NP]
            y_s[c * 16 + p * 8:c * 16 + (p + 1) * 8] = y[p, NP:].reshape(8, 4, D)
    cat = lambda name, ax: np.ascontiguousarray(np.concatenate([R[c][name] for c in range(NCORES)], axis=ax))
    stk = lambda name: np.ascontiguousarray(np.stack([R[c][name] for c in range(NCORES)], axis=1))
    return (y_p, y_s, stk('retp'), cat('rets', 1), stk('hgp'), cat('hgs', 1),
            stk('lrup'), cat('lrus', 1), stk('convp'), cat('convs', 1))
```

```python
import os
import numpy as np
import concourse.bass as bass
import concourse.mybir as mybir
from concourse.bass_utils import run_bass_kernel_spmd
from contextlib import ExitStack

F32 = mybir.dt.float32
BF16 = mybir.dt.bfloat16
AF = mybir.ActivationFunctionType
ALU = mybir.AluOpType

D = 1024
SEQ = 2048
DEPTH = 2
NCORES = 8
PAST_LEN = 16384
FF = 2816
NP = 1024
NSQ = 8
NS = NSQ * 4
N = NP + NS
TT3 = [(0, 352), (352, 704), (704, 1056)]
TK = [(i * 128, 128) for i in range(8)] + [(1024, 32)]
EPS = 1e-6
KD = 8
NW = 2
NSCR = 8

VN = ['ffn1_norm', 'mix_norm', 'ffn2_norm', 'mb0', 'mb1', 'mb2', 'lb_logits', 'hgrn_norm',
      'cw0', 'cw1', 'cw2', 'cw3', 'conv_b', 'lru_b_a', 'lru_b_x', 'lru_lambda']
NVL = len(VN)
NV = 2 * NVL + 1


def VI(l, name):
    return l * NVL + VN.index(name)


def tts_of(c0, c1):
    return [i for i, (a, b) in enumerate(TT3) if a < c1 and c0 < b]


class Op:
    __slots__ = ('eng', 'fn', 'deps', 'idx', 'dma', 'sig', 'sem', 'val', 'ndep', 'raw')


class Sched:
    def __init__(self):
        self.ops = []
        self.last_w = {}
        self.readers = {}

    def add(self, eng, fn, reads=(), writes=(), dma=False):
        op = Op()
        op.eng = eng; op.fn = fn; op.dma = dma; op.idx = len(self.ops)
        op.sig = None; op.sem = None; op.val = None; op.ndep = 0
        deps = set()
        reads = [k for x in reads for k in (x if isinstance(x, list) else [x])]
        writes = [k for x in writes for k in (x if isinstance(x, list) else [x])]
        lw = self.last_w; rd = self.readers
        for k in reads:
            w = lw.get(k)
            if w is not None:
                deps.add(w)
        op.raw = set(deps)
        for k in writes:
            w = lw.get(k)
            if w is not None:
                deps.add(w)
            r = rd.get(k)
            if r:
                deps.update(r)
        for k in reads:
            rd.setdefault(k, []).append(op.idx)
        for k in writes:
            lw[k] = op.idx
            rd[k] = []
        deps.discard(op.idx)
        op.deps = deps
        self.ops.append(op)
        return op


def build_program(dbg=None):
    dbg = dbg or {}
    nc = bass.Bass("TRN2", target_bir_lowering=False)
    KD = dbg.get('kd', 4)
    S = Sched()
    marks = []

    def MARK(name):
        marks.append((name, sum(1 for o in S.ops if o.eng == 'pe')))
    es = ExitStack()

    def din(name, shape, dt=F32):
        return nc.dram_tensor(name, list(shape), dt, kind="ExternalInput").ap()

    def dout(name, shape, dt=F32):
        return nc.dram_tensor(name, list(shape), dt, kind="ExternalOutput").ap()

    def sb(name, shape, dt=F32):
        return es.enter_context(nc.sbuf_tensor(name, list(shape), dt))

    xin = din("xin", [2, N, D])
    sret = din("sret", [2, 16, 4, 128, 256])
    shg = din("shg", [2, 16, 8, 128, 128])
    slru = din("slru", [2, 16, D])
    sconv = din("sconv", [2, 16, 3, D])
    wg_d = [din("ffn1_w_gate", [2, D, FF]), din("ffn2_w_gate", [2, D, FF])]
    wu_d = [din("ffn1_w_up", [2, D, FF]), din("ffn2_w_up", [2, D, FF])]
    wd_d = [din("ffn1_w_down", [2, FF, D]), din("ffn2_w_down", [2, FF, D])]
    win_d = din("w_in", [2, D, 12288])
    wro_d = din("w_ret_o", [2, D, D])
    who_d = din("w_hgrn_o", [2, D, D])
    wlo_d = din("w_lru_o", [2, D, D])
    wmo_d = din("w_mix_out", [2, D, D])
    lwa_d = din("lru_w_a", [2, 8, 128, 128])
    lwx_d = din("lru_w_x", [2, 8, 128, 128])
    vecs_d = din("vecs", [128, NV, 8])
    cs_d = din("tab_cs", [2, 128, 9, 2, 64])
    dec_d = din("tab_dec", [128, 2, 3, 4])
    mk_d = din("tab_mask", [128, 64])
    mks_d = din("tab_masks", [32, 32])
    oh_d = din("tab_oh", [32, 8])
    id_d = din("tab_ident", [128, 128])

    y_o = dout("y", [2, N, D])
    retp_o = dout("retp", [2, 4, 128, 256])
    rets_o = dout("rets", [2, 16, 4, 128, 256])
    hgp_o = dout("hgp", [2, 8, 128, 128])
    hgs_o = dout("hgs", [2, 16, 8, 128, 128])
    lrup_o = dout("lrup", [2, D])
    lrus_o = dout("lrus", [2, 16, D])
    convp_o = dout("convp", [2, 3, D])
    convs_o = dout("convs", [2, 16, 3, D])

    xT = sb("xT", [128, 8, N])
    hT = sb("hT", [128, 8, N], BF16)
    ob = sb("ob", [128, 8, N], BF16)
    gt = sb("gt", [128, 8, N], BF16)
    wst = [sb("wst%d" % i, [128, 8, 256]) for i in range(NW)]
    wbf = [sb("wbf%d" % i, [128, 8, 256], BF16) for i in range(NW)]
    rstd = sb("rstd", [128, N])
    sqt = [sb("sqt%d" % i, [128, 352], BF16) for i in range(4)]
    tmpA = [sb("tmpA%d" % i, [128, 352]) for i in range(3)]
    Tbig = sb("Tbig", [128, NSCR, N])
    T = [Tbig[:, i, :] for i in range(NSCR)]
    XLp = T[7]
    vbig_r = T[7][0:32, :].bitcast(BF16)[:, 0:2048].rearrange("p (b v) -> p b v", b=8)
    vbig_h = T[0][0:32, :].bitcast(BF16)[:, 0:2048].rearrange("p (b v) -> p b v", b=8)
    QT = sb("QT", [128, 2, N], BF16)
    KT = sb("KT", [128, 2, N], BF16)
    K2 = sb("K2", [128, 9, 256], BF16)
    Vt = sb("Vt", [128, 9, 256], BF16)
    SR = [[sb("SR%d_%d" % (l, h), [128, 256]) for h in range(4)] for l in range(2)]
    SH = [[sb("SH%d_%d" % (l, h), [128, 128]) for h in range(8)] for l in range(2)]
    SIN = sb("SIN", [128, 1024])
    SOUT = sb("SOUT", [128, 1024])
    SR16S = sb("SR16S", [128, 1024], BF16)
    scm = [sb("scm%d" % i, [128, 64], BF16) for i in range(2)]
    scm_all = sb("scm_all", [128, 8, 64], BF16)
    SRalt = sb("SRalt", [128, 256])
    dummy = sb("fence_dummy", [128, 2])
    cs_t = sb("cs_t", [128, 9, 2, 64])
    dec_t = sb("dec_t", [128, 2, 3, 4])
    mk_t = sb("mk_t", [128, 64])
    mks_t = sb("mks_t", [32, 32])
    oh_t = sb("oh_t", [32, 8])
    id32 = sb("id32", [128, 128])
    id16 = sb("id16", [128, 128], BF16)
    ones16 = sb("ones16", [128, 128], BF16)
    ones_c = sb("ones_c", [128, 2])
    vecs = sb("vecs_t", [128, NV, 8])
    lb_t = sb("lb_t", [128, 2, 8])
    omlb_t = sb("omlb_t", [128, 2, 8])
    m8sp = sb("m8sp", [128, 2, 8])
    carry_h = sb("carry_h", [128, 2, 8])
    carry_c = sb("carry_c", [128, 2, 8, 3])
    XLs = sb("XLs", [128, NSQ, 7])
    XLs2 = sb("XLs2", [128, NSQ, 7])
    SCt = sb("SCt", [128, 8, 24])
    SLt = sb("SLt", [128, 8, 8])
    CN = sb("CN", [128, 8, 27])
    HN = sb("HN", [128, 8, 9])
    er2 = sb("er2", [128, 2, 24])
    eb2 = sb("eb2", [128, 2, 24])
    ebr2 = sb("ebr2", [128, 2, 24])
    hs_t = sb("hs_t", [128, NSQ])
    hs_t2 = sb("hs_t2", [128, NSQ])
    ytok = [T[0], T[1]]
    tok_a = Vt[:].rearrange("p a b -> p (a b)").bitcast(F32)
    tok_b = QT[:].rearrange("p a b -> p (a b)").bitcast(F32)
    TOKA = [('V', kk) for kk in range(9)]
    TOKB = [('QT', tt_) for tt_ in range(3)]

    ps = [es.enter_context(nc.psum_tensor("ps%d" % i, [128, 512], F32)) for i in range(8)]
    ps_ctr = [0]

    lps_ctr = [0]

    def bank():
        b = ps_ctr[0] % 6
        ps_ctr[0] += 1
        return b

    def lbank():
        b = 6 + lps_ctr[0] % 2
        lps_ctr[0] += 1
        return b

    def MM(out, lhsT, rhs, start, stop, r, w):
        S.add('pe', lambda e: e.matmul(out, lhsT=lhsT, rhs=rhs, start=start, stop=stop), r, w)

    def TR(out, in_, ident, r, w):
        S.add('pe', lambda e: e.transpose(out=out, in_=in_, identity=ident), r, w)

    def ACT(out, in_, func, r, w, bias=None, scale=None):
        kw = {}
        if bias is not None:
            kw['bias'] = bias
        if scale is not None:
            kw['scale'] = scale
        S.add('act', lambda e: e.activation(out=out, in_=in_, func=func, **kw), r, w)

    def TTo(eng, out, a, b, op, r, w):
        S.add(eng, lambda e: e.tensor_tensor(out=out, in0=a, in1=b, op=op), r, w)

    def TS(eng, out, a, s1, s2, op0, op1, r, w):
        if op1 is None and eng == 'pool':
            S.add(eng, lambda e: e.tensor_scalar(out=out, in0=a, scalar1=s1, scalar2=0.0, op0=op0, op1=ALU.add), r, w)
        elif op1 is None:
            S.add(eng, lambda e: e.tensor_scalar(out=out, in0=a, scalar1=s1, scalar2=None, op0=op0), r, w)
        else:
            S.add(eng, lambda e: e.tensor_scalar(out=out, in0=a, scalar1=s1, scalar2=s2, op0=op0, op1=op1), r, w)

    def STT(out, a, s, b, op0, op1, r, w):
        S.add('dve', lambda e: e.scalar_tensor_tensor(out=out, in0=a, scalar=s, in1=b, op0=op0, op1=op1), r, w)

    def CP(eng, out, in_, r, w):
        if eng == 'act':
            S.add('act', lambda e: e.copy(out=out, in_=in_), r, w)
        else:
            S.add(eng, lambda e: e.tensor_copy(out=out, in_=in_), r, w)

    def MSET(eng, ap, val, w):
        S.add(eng, lambda e: e.memset(ap, val), (), w)

    def DMA(out, in_, r, w, slow=False):
        if slow:
            S.add('sp', lambda e: e.dma_start(out=out, in_=in_, allow_slow_non_contiguous=True), r, w, dma=True)
        else:
            S.add('sp', lambda e: e.dma_start(out=out, in_=in_), r, w, dma=True)

    def FENCE(rkeys, wkeys):
        S.add('dve', lambda e: e.memset(dummy[:], 0.0), list(rkeys) + ['fence_dummy'], list(wkeys) + ['fence_dummy'])

    def kx(kc, tts):
        return [('x', kc, t) for t in tts]

    def kall(name, *idx):
        return [(name,) + tuple(idx) + (t,) for t in range(3)]

    wctr = [0]
    sctr = [0]
    ring_small = ([(wst[i], [('wst', i)]) for i in range(NW)], [(wbf[i], [('wbf', i)]) for i in range(NW)])
    Tflat = Tbig[:].rearrange("p a b -> p (a b)")
    stg_big = list(ring_small[0])
    for j in range(4):
        stg_big.append((Tflat[:, 2 * N * j:2 * N * j + 2048].rearrange("p (k c) -> p k c", k=8),
                        [('T', i_, t_) for i_ in (2 * j, 2 * j + 1) for t_ in range(3)]))
    bf_big = list(ring_small[1])
    bf_big.append((QT[:].rearrange("p a b -> p (a b)")[:, 0:2048].rearrange("p (k c) -> p k c", k=8), [('QT', t_) for t_ in range(3)]))
    bf_big.append((KT[:].rearrange("p a b -> p (a b)")[:, 0:2048].rearrange("p (k c) -> p k c", k=8), [('KT', t_) for t_ in range(3)]))
    bf_big.append((K2[:].rearrange("p a b -> p (a b)")[:, 0:2048].rearrange("p (k c) -> p k c", k=8), [('K2', k_) for k_ in range(9)]))
    bf_big.append((Vt[:].rearrange("p a b -> p (a b)")[:, 0:2048].rearrange("p (k c) -> p k c", k=8), [('V', k_) for k_ in range(9)]))
    ring_big = (stg_big, bf_big)
    ring = [ring_small]
    cast_pat = dbg.get('cast_pat', ['dve'])
    cast_big = dbg.get('cast_big', ['pool', 'act'])

    def wtile(src, kcn, cols=256):
        i = wctr[0]
        wctr[0] += 1
        stgs, bfs = ring[0]
        st_t, st_k = stgs[sctr[0] % len(stgs)]
        bf_t, bf_k = bfs[i % len(bfs)]
        sctr[0] += 1
        DMA(st_t[:, 0:kcn, 0:cols], src.rearrange("(kc p) n -> p kc n", p=128), (), [st_k])
        pat = cast_big if ring[0] is ring_big else cast_pat
        eng = pat[i % len(pat)]
        CP(eng, bf_t[:, 0:kcn, 0:cols], st_t[:, 0:kcn, 0:cols], [st_k], [bf_k])
        return bf_t, bf_k

    class WStream:
        def __init__(self, reqs, pd=dbg.get('pd', 3)):
            self.reqs = reqs; self.pd = pd; self.issued = []

        def get(self, i):
            while len(self.issued) < min(len(self.reqs), i + 1 + self.pd):
                src, kcn, cols = self.reqs[len(self.issued)]
                self.issued.append(wtile(src, kcn, cols))
            return self.issued[i]

    DMA(vecs[:], vecs_d, (), ['vecs'])
    DMA(cs_t[:], cs_d[0], (), ['cs'])
    DMA(dec_t[:], dec_d, (), ['dec'])
    DMA(mk_t[:], mk_d, (), ['mk'])
    DMA(mks_t[:], mks_d, (), ['mks'])
    DMA(oh_t[:], oh_d, (), ['oh'])
    DMA(id32[:], id_d, (), ['id32'])
    CP('dve', id16[:], id32[:], ['id32'], ['id16'])
    MSET('dve', ones16[:], 1.0, ['ones16'])
    MSET('pool', ones_c[:], 1.0, ['ones32'])
    MSET('dve', lb_t[:, 0, :], 0.0, ['lb'])
    TTo('dve', lb_t[:, 1, :], vecs[:, VI(1, 'lb_logits'), :], vecs[:, VI(0, 'lb_logits'), :], ALU.subtract, ['vecs', 'lb'], ['lb'])
    ACT(lb_t[:, 1, :], lb_t[:, 1, :], AF.Sigmoid, ['lb'], ['lb'])
    TS('dve', omlb_t[:], lb_t[:], -1.0, 1.0, ALU.mult, ALU.add, ['lb'], ['omlb'])
    for l in range(2):
        ACT(m8sp[:, l, :], vecs[:, VI(l, 'lru_lambda'), :], AF.Exp, ['vecs', 'm8sp'], ['m8sp'], scale=-1.0)
    ACT(m8sp[:], m8sp[:], AF.Ln, ['m8sp'], ['m8sp'], bias=1.0)
    TS('dve', m8sp[:], m8sp[:], -8.0, None, ALU.mult, None, ['m8sp'], ['m8sp'])
    for l in range(2):
        for h in range(4):
            MSET('pool', SR[l][h][:], 0.0, [('SR', l, h)])
        for h in range(8):
            MSET('pool', SH[l][h][:], 0.0, [('SH', l, h)])
    MSET('dve', carry_h[:], 0.0, ['carry_h'])
    MSET('dve', carry_c[:], 0.0, ['carry_c'])

    def emit_norm(vidx):
        for ti, (c0, c1) in enumerate(TT3):
            b = bank()
            for kc in range(8):
                sq = sqt[kc % 4]
                ACT(sq[:, 0:c1 - c0], xT[:, kc, c0:c1], AF.Square, [('x', kc, ti)], [('sqt', kc % 4)])
                MM(ps[b][:, 0:c1 - c0], ones16[:], sq[:, 0:c1 - c0], kc == 0, kc == 7,
                   [('sqt', kc % 4), 'ones16'], [('ps', b)])
            ACT(rstd[:, c0:c1], ps[b][:, 0:c1 - c0], AF.Ln, [('ps', b)], [('rs', ti)], bias=EPS, scale=1.0 / D)
            ACT(rstd[:, c0:c1], rstd[:, c0:c1], AF.Exp, [('rs', ti)], [('rs', ti)], scale=-0.5)
            for kc in range(8):
                STT(hT[:, kc, c0:c1], xT[:, kc, c0:c1], vecs[:, vidx, kc:kc + 1], rstd[:, c0:c1],
                    ALU.mult, ALU.mult, [('x', kc, ti), ('rs', ti), 'vecs'], [('h', kc, ti)])

    def fm_proj(wt, wkey, mloc, src, skey, kcn, consume):
        for ti, (c0, c1) in enumerate(TT3):
            b = bank()
            for kc in range(kcn):
                MM(ps[b][:, 0:c1 - c0], wt[:, kc, mloc * 128:(mloc + 1) * 128], src[:, kc, c0:c1],
                   kc == 0, kc == kcn - 1, [wkey, (skey, kc, ti)], [('ps', b)])
            consume(ti, c0, c1, b)

    def emit_ffn(l, which):
        wg, wu, wd = wg_d[which][l], wu_d[which][l], wd_d[which][l]
        ring[0] = ring_big
        groups = [(0, 6), (6, 12), (12, 17), (17, 22)]
        reqs = []
        for (j0, j1) in groups:
            j = j0
            while j < j1:
                nj = min(2, j1 - j)
                reqs.append((wg[:, j * 128:(j + nj) * 128], 8, nj * 128))
                reqs.append((wu[:, j * 128:(j + nj) * 128], 8, nj * 128))
                j += nj
            for mp in range(4):
                reqs.append((wd[j0 * 128:j1 * 128, mp * 256:(mp + 1) * 256], j1 - j0, 256))
        ws = WStream(reqs)
        wi = [0]

        def nxt():
            r = ws.get(wi[0])
            wi[0] += 1
            return r
        for (j0, j1) in groups[:dbg.get('ffn_groups', 4)]:
            j = j0
            while j < j1:
                nj = min(2, j1 - j)
                wgt, wgk = nxt()
                wut, wuk = nxt()
                for m in range(nj):
                    jj = j + m - j0
                    for ti, (c0, c1) in enumerate(TT3):
                        w_ = c1 - c0
                        bg = bank()
                        for kc in range(8):
                            MM(ps[bg][:, 0:w_], wgt[:, kc, m * 128:(m + 1) * 128], hT[:, kc, c0:c1], kc == 0, kc == 7,
                               [wgk, ('h', kc, ti)], [('ps', bg)])
                        bu = bank()
                        for kc in range(8):
                            MM(ps[bu][:, 0:w_], wut[:, kc, m * 128:(m + 1) * 128], hT[:, kc, c0:c1], kc == 0, kc == 7,
                               [wuk, ('h', kc, ti)], [('ps', bu)])
                        ta = tmpA[ti % 3]
                        ACT(ta[:, 0:w_], ps[bg][:, 0:w_], AF.Silu, [('ps', bg)], [('tmpA', ti % 3)])
                        TTo('dve', gt[:, jj, c0:c1], ps[bu][:, 0:w_], ta[:, 0:w_], ALU.mult,
                            [('ps', bu), ('tmpA', ti % 3)], [('g', jj, ti)])
                j += nj
            ng = j1 - j0
            for mp in range(4 if dbg.get('ffn_b', 1) else 0):
                wdt, wdk = nxt()
                for m in range(2):
                    mi = mp * 2 + m
                    for ti, (c0, c1) in enumerate(TT3):
                        w_ = c1 - c0
                        b = bank()
                        for jj in range(ng):
                            MM(ps[b][:, 0:w_], wdt[:, jj, m * 128:(m + 1) * 128], gt[:, jj, c0:c1], jj == 0, jj == ng - 1,
                               [wdk, ('g', jj, ti)], [('ps', b)])
                        STT(xT[:, mi, c0:c1], ps[b][:, 0:w_], 0.5, xT[:, mi, c0:c1], ALU.mult, ALU.add,
                            [('ps', b), ('x', mi, ti)], [('x', mi, ti)])
        ring[0] = ring_small

    def head_norm_cols(srcs, nfeat, ti, c0, c1):
        w_ = c1 - c0
        b = bank()
        for i, (apf, key) in enumerate(srcs):
            sq = sqt[i % 2]
            ACT(sq[:, 0:w_], apf(c0, c1), AF.Square, [key + (ti,)], [('sqt', i % 2)])
            MM(ps[b][:, 0:w_], ones16[:], sq[:, 0:w_], i == 0, i == len(srcs) - 1, [('sqt', i % 2), 'ones16'], [('ps', b)])
        ACT(rstd[:, c0:c1], ps[b][:, 0:w_], AF.Ln, [('ps', b)], [('rs', ti)], bias=EPS, scale=1.0 / nfeat)
        ACT(rstd[:, c0:c1], rstd[:, c0:c1], AF.Exp, [('rs', ti)], [('rs', ti)], scale=-0.5)

    def emit_branch_out(l, bi, wo_d):
        ring[0] = ring_big
        reqs = []
        for mp in range(4):
            reqs.append((wo_d[l][:, mp * 256:(mp + 1) * 256], 8, 256))
            reqs.append((win_d[l][:, 9216 + bi * 1024 + mp * 256: 9216 + bi * 1024 + (mp + 1) * 256], 8, 256))
        for mp in range(4):
            reqs.append((wmo_d[l][:, mp * 256:(mp + 1) * 256], 8, 256))
        ws = WStream(reqs)
        for mp in range(4):
            wot, wok = ws.get(2 * mp)
            wgt, wgk = ws.get(2 * mp + 1)
            for m in range(2):
                mi = mp * 2 + m
                for ti, (c0, c1) in enumerate(TT3):
                    w_ = c1 - c0
                    bb = bank()
                    for kc in range(8):
                        MM(ps[bb][:, 0:w_], wot[:, kc, m * 128:(m + 1) * 128], ob[:, kc, c0:c1], kc == 0, kc == 7,
                           [wok, ('o', kc, ti)], [('ps', bb)])
                    bg = bank()
                    for kc in range(8):
                        MM(ps[bg][:, 0:w_], wgt[:, kc, m * 128:(m + 1) * 128], hT[:, kc, c0:c1], kc == 0, kc == 7,
                           [wgk, ('h', kc, ti)], [('ps', bg)])
                    ta = tmpA[ti % 3]
                    ACT(ta[:, 0:w_], ps[bg][:, 0:w_], AF.Sigmoid, [('ps', bg), 'vecs'], [('tmpA', ti % 3)],
                        bias=vecs[:, VI(l, 'mb%d' % bi), mi:mi + 1])
                    TTo('dve', gt[:, mi, c0:c1], ps[bb][:, 0:w_], ta[:, 0:w_], ALU.mult,
                        [('ps', bb), ('tmpA', ti % 3)], [('g', mi, ti)])
        for mp in range(4):
            wmt, wmk = ws.get(8 + mp)
            for m in range(2):
                mi = mp * 2 + m
                for ti, (c0, c1) in enumerate(TT3):
                    w_ = c1 - c0
                    b = bank()
                    for kc in range(8):
                        MM(ps[b][:, 0:w_], wmt[:, kc, m * 128:(m + 1) * 128], gt[:, kc, c0:c1], kc == 0, kc == 7,
                           [wmk, ('g', kc, ti)], [('ps', b)])
                    TTo('dve', xT[:, mi, c0:c1], ps[b][:, 0:w_], xT[:, mi, c0:c1], ALU.add,
                        [('ps', b), ('x', mi, ti)], [('x', mi, ti)])
        ring[0] = ring_small

    def emit_lru(l, p):
        for wi, wdram in enumerate((lwa_d, lwx_d)):
            st_t, st_k = ring[0][0][sctr[0] % len(ring[0][0])]
            sctr[0] += 1
            DMA(st_t[:, 0:8, 0:128], wdram[l].rearrange("n d e -> d n e"), (), [st_k])
            CP('dve', K2[:, 0:8, wi * 128:(wi + 1) * 128], st_t[:, 0:8, 0:128], [st_k], [('K2', kk) for kk in range(9)])
        DMA(tok_a[0:24, 0:D], sconv[l, p * 8:(p + 1) * 8].rearrange("b j w -> (b j) w"), (), TOKA)
        for half in range(2):
            b = bank()
            for q in range(4):
                n = half * 4 + q
                TR(ps[b][:, q * 32:q * 32 + 24], tok_a[0:24, n * 128:(n + 1) * 128], id32[0:24, 0:24], TOKA + ['id32'], [('ps', b)])
            CP('dve', SCt[:, half * 4:(half + 1) * 4, :], ps[b][:, 0:128].rearrange("p (q r) -> p q r", r=32)[:, :, 0:24],
               [('ps', b)], ['SCt'])
        DMA(tok_a[0:8, 0:D], slru[l, p * 8:(p + 1) * 8], TOKA, TOKA)
        for half in range(2):
            b = bank()
            for q in range(4):
                n = half * 4 + q
                TR(ps[b][:, q * 32:q * 32 + 8], tok_a[0:8, n * 128:(n + 1) * 128], id32[0:8, 0:8], TOKA + ['id32'], [('ps', b)])
            CP('dve', SLt[:, half * 4:(half + 1) * 4, :], ps[b][:, 0:128].rearrange("p (q r) -> p q r", r=32)[:, :, 0:8],
               [('ps', b)], ['SLt'])

        G32 = gt[:].rearrange("p a b -> p (a b)").bitcast(F32)

        def gk(i):
            return lambda t=None: [('g', 2 * i + j_, t_) for j_ in range(2) for t_ in (range(3) if t is None else [t])]

        def tk(i):
            return lambda t=None: [('T', i, t_) for t_ in (range(3) if t is None else [t])]

        def nk(name, cnt=3):
            return lambda t=None: [(name, t_) for t_ in (range(cnt) if t is None else [t])]
        B0 = dict(XLp=T[7], kXLp=tk(7), XC=T[0], kXC=tk(0), XCb=T[1][:].bitcast(BF16), kXCb=tk(1), R=T[2], kR=tk(2),
                  I=T[3], kI=tk(3), A=T[4], kA=tk(4), HS=T[6], kHS=tk(6), XLs=XLs, kXLs='XLs', hs=hs_t, khs='hs_t')
        B1 = dict(XLp=G32[:, 0:N], kXLp=gk(0), XC=G32[:, N:2 * N], kXC=gk(1),
                  XCb=Vt[:].rearrange("p a b -> p (a b)"), kXCb=(lambda t=None: [('V', k_) for k_ in range(9)]),
                  R=G32[:, 2 * N:3 * N], kR=gk(2), I=G32[:, 3 * N:4 * N], kI=gk(3),
                  A=QT[:].rearrange("p a b -> p (a b)").bitcast(F32), kA=nk('QT'),
                  HS=KT[:].rearrange("p a b -> p (a b)").bitcast(F32), kHS=nk('KT'), XLs=XLs2, kXLs='XLs2', hs=hs_t2, khs='hs_t2')

        def chunk_gen(n, m, wxt, wxk, wgt, wgk, B):
            XLp_, XC, XCb16, R, I, A, HS, XLs_, hs_ = B['XLp'], B['XC'], B['XCb'], B['R'], B['I'], B['A'], B['HS'], B['XLs'], B['hs']
            kXLp, kXC, kXCb, kR, kI, kA, kHS, kXLs, khs = B['kXLp'], B['kXC'], B['kXCb'], B['kR'], B['kI'], B['kA'], B['kHS'], B['kXLs'], B['khs']
            CP('dve', XLp_[:, 0:3], carry_c[:, l, n, :], ['carry_c'] + kXLp(), kXLp())
            CP('dve', XLs_[:, :, 0:3], SCt[:, n, :].rearrange("p (b j) -> p b j", j=3), ['SCt', kXLs], [kXLs])

            def cons_x(ti, c0, c1, b):
                pe = min(c1, NP)
                if pe > c0:
                    CP('act', XLp_[:, 3 + c0:3 + pe], ps[b][:, 0:pe - c0], [('ps', b)] + kXLp(), kXLp())
                if c1 > NP:
                    CP('act', XLs_[:, :, 3:7], ps[b][:, NP - c0:c1 - c0].rearrange("p (b t) -> p b t", t=4),
                       [('ps', b), kXLs], [kXLs])
            fm_proj(wxt, wxk, m, hT, 'h', 8, cons_x)
            yield
            cw = [vecs[:, VI(l, 'cw%d' % i), n:n + 1] for i in range(4)]
            cb = vecs[:, VI(l, 'conv_b'), n:n + 1]
            xcp = XC[:, 0:NP]
            xcs = XC[:, NP:N].rearrange("p (b t) -> p b t", t=4)
            TS('dve', xcp, XLp_[:, 3:3 + NP], cw[3], cb, ALU.mult, ALU.add, kXLp() + ['vecs'], kXC())
            TS('dve', xcs, XLs_[:, :, 3:7], cw[3], cb, ALU.mult, ALU.add, [kXLs, 'vecs'], kXC())
            for i in (2, 1, 0):
                STT(xcp, XLp_[:, i:i + NP], cw[i], xcp, ALU.mult, ALU.add, kXLp() + ['vecs'] + kXC(), kXC())
                STT(xcs, XLs_[:, :, i:i + 4], cw[i], xcs, ALU.mult, ALU.add, [kXLs, 'vecs'] + kXC(), kXC())
            yield
            CP('pool', CN[:, n, 0:24].rearrange("p (b j) -> p b j", j=3), XLs_[:, :, 4:7], [kXLs, 'CN'], ['CN'])
            CP('pool', CN[:, n, 24:27], XLp_[:, NP:NP + 3], kXLp() + ['CN'], ['CN'])
            CP('pool', carry_c[:, l, n, :], XLp_[:, NP:NP + 3], kXLp() + ['carry_c'], ['carry_c'])
            CP('act', XCb16[:, 0:N], XC[:, :], kXC(), kXCb())
            yield
            for gi, (dst, kd, bname) in enumerate(((R, kR, 'lru_b_a'), (I, kI, 'lru_b_x'))):
                for ti, (c0, c1) in enumerate(TT3):
                    b = bank()
                    MM(ps[b][:, 0:c1 - c0], K2[:, n, gi * 128:(gi + 1) * 128], XCb16[:, c0:c1], True, True, [('K2', n)] + kXCb(ti), [('ps', b)])
                    ACT(dst[:, c0:c1], ps[b][:, 0:c1 - c0], AF.Sigmoid, [('ps', b), 'vecs'], kd(ti),
                        bias=vecs[:, VI(l, bname), n:n + 1])
            yield
            ACT(A[:, :], R[:, :], AF.Exp, kR() + ['m8sp'], kA(), scale=m8sp[:, l, n:n + 1])
            ACT(R[:, :], A[:, :], AF.Square, kA(), kR())
            ACT(R[:, :], R[:, :], AF.Sqrt, kR(), kR(), bias=1.0, scale=-1.0)
            yield
            TTo('dve', I[:, :], I[:, :], XC[:, :], ALU.mult, kI() + kXC(), kI())
            TTo('dve', I[:, :], I[:, :], R[:, :], ALU.mult, kI() + kR(), kI())
            U = I
            S.add('dve', lambda e, n=n: e.tensor_tensor_scan(out=HS[:, 0:NP], data0=A[:, 0:NP], data1=U[:, 0:NP],
                                                             initial=carry_h[:, l, n:n + 1], op0=ALU.mult, op1=ALU.add),
                  kA() + kI() + ['carry_h'], kHS())
            yield
            a_s = A[:, NP:N].rearrange("p (b t) -> p b t", t=4)
            u_s = U[:, NP:N].rearrange("p (b t) -> p b t", t=4)
            h_s = HS[:, NP:N].rearrange("p (b t) -> p b t", t=4)
            for t in range(4):
                prev = SLt[:, n, :] if t == 0 else h_s[:, :, t - 1]
                TTo('dve', hs_[:, :], a_s[:, :, t], prev, ALU.mult, kA() + kHS() + ['SLt', khs], [khs])
                TTo('dve', h_s[:, :, t], hs_[:, :], u_s[:, :, t], ALU.add, [khs] + kI() + kHS(), kHS())
            CP('pool', HN[:, n, 0:8], h_s[:, :, 3], kHS() + ['HN'], ['HN'])
            CP('pool', HN[:, n, 8:9], HS[:, NP - 1:NP], kHS() + ['HN'], ['HN'])
            CP('pool', carry_h[:, l, n:n + 1], HS[:, NP - 1:NP], kHS() + ['carry_h'], ['carry_h'])
            yield

            def cons_g(ti, c0, c1, b):
                ta = tmpA[ti % 3]
                ACT(ta[:, 0:c1 - c0], ps[b][:, 0:c1 - c0], AF.Gelu, [('ps', b)], [('tmpA', ti % 3)])
                TTo('dve', ob[:, n, c0:c1], HS[:, c0:c1], ta[:, 0:c1 - c0], ALU.mult,
                    kHS(ti) + [('tmpA', ti % 3)], [('o', n, ti)])
            fm_proj(wgt, wgk, m, hT, 'h', 8, cons_g)
            yield
        for t2 in range(4):
            wxt, wxk = wtile(win_d[l][:, 7168 + t2 * 256:7168 + (t2 + 1) * 256], 8)
            wgt, wgk = wtile(win_d[l][:, 8192 + t2 * 256:8192 + (t2 + 1) * 256], 8)
            gens = [chunk_gen(t2 * 2, 0, wxt, wxk, wgt, wgk, B0), chunk_gen(t2 * 2 + 1, 1, wxt, wxk, wgt, wgk, B1)]
            live = list(gens)
            while live:
                for g_ in list(live):
                    try:
                        next(g_)
                    except StopIteration:
                        live.remove(g_)
        for half in range(2):
            b = bank()
            for q in range(4):
                n = half * 4 + q
                TR(ps[b][0:27, q * 128:(q + 1) * 128], CN[:, n, :], id32[:, :], ['CN', 'id32'], [('ps', b)])
            CP('dve', tok_b[0:27, half * 512:(half + 1) * 512], ps[b][0:27, :], [('ps', b)] + TOKB, TOKB)
        DMA(convs_o[l, p * 8:(p + 1) * 8].rearrange("b j w -> (b j) w"), tok_b[0:24, 0:D], TOKB, [('out', 'convs', l, p)] + TOKB)
        if p == 1:
            DMA(convp_o[l], tok_b[24:27, 0:D], TOKB, [('out', 'convp', l)] + TOKB)
        for half in range(2):
            b = bank()
            for q in range(4):
                n = half * 4 + q
                TR(ps[b][0:9, q * 128:(q + 1) * 128], HN[:, n, :], id32[:, :], ['HN', 'id32'], [('ps', b)])
            CP('dve', tok_b[0:9, half * 512:(half + 1) * 512], ps[b][0:9, :], [('ps', b)] + TOKB, TOKB)
        DMA(lrus_o[l, p * 8:(p + 1) * 8], tok_b[0:8, 0:D], TOKB, [('out', 'lrus', l, p)] + TOKB)
        if p == 1:
            DMA(lrup_o[l:l + 1, :], tok_b[8:9, 0:D], TOKB, [('out', 'lrup', l)] + TOKB)
        MARK('p%d l%d lru_out' % (p, l))
        emit_branch_out(l, 2, wlo_d)

    def tm_proj(wt, wkey, ncols, k, consume_bank):
        c0, R = TK[k]
        b = bank()
        tts = tts_of(c0, c0 + R)
        for kc in range(8):
            MM(ps[b][0:R, 0:ncols], hT[:, kc, c0:c0 + R], wt[:, kc, 0:ncols], kc == 0, kc == 7,
               [wkey] + [('h', kc, t) for t in tts], [('ps', b)])
        consume_bank(k, R, b)

    def emit_ret(l, p):
        gam = [1.0 - 2.0 ** (-5 - h) for h in range(4)]
        t1, t2, t3, t4, rot = T[0], T[1], T[2], T[3], T[4]
        OT = [T[5], T[6]]
        for hp in range(2):
            wqt, wqk = wtile(win_d[l][:, hp * 256:(hp + 1) * 256], 8)
            for which in range(2):
                if which == 1:
                    wqt, wqk = wtile(win_d[l][:, 512 + hp * 256:512 + (hp + 1) * 256], 8)

                pending = []

                def cons_qk(k, R, b, which=which):
                    while pending:
                        pending.pop(0)()
                    kind = 0 if k < 8 else 1
                    pv = ps[b][0:R, 0:256].rearrange("p (h two j) -> p h two j", h=2, two=2)
                    x1 = pv[:, :, 0, :]
                    x2 = pv[:, :, 1, :]
                    cosb = cs_t[0:R, k, 0, :].unsqueeze(1).broadcast_to([R, 2, 64])
                    sinb = cs_t[0:R, k, 1, :].unsqueeze(1).broadcast_to([R, 2, 64])
                    v = lambda t_: t_[0:R, 0:128].rearrange("p (h j) -> p h j", h=2)
                    rk = [('ps', b), 'cs']
                    TTo('dve', v(t1), x1, cosb, ALU.mult, rk, [('T', 0, 0)])
                    TTo('dve', v(t2), x2, sinb, ALU.mult, rk, [('T', 1, 0)])
                    TTo('dve', v(t3), x1, sinb, ALU.mult, rk, [('T', 2, 0)])
                    TTo('dve', v(t4), x2, cosb, ALU.mult, rk, [('T', 3, 0)])
                    rv = rot[0:R, 0:256].rearrange("p (h two j) -> p h two j", h=2, two=2)
                    TTo('dve', rv[:, :, 0, :], v(t1), v(t2), ALU.subtract, [('T', 0, 0), ('T', 1, 0)], [('T', 4, 0)])
                    TTo('dve', rv[:, :, 1, :], v(t3), v(t4), ALU.add, [('T', 2, 0), ('T', 3, 0)], [('T', 4, 0)])
                    rv2 = rot[0:R, 0:256].rearrange("p (h d) -> p h d", h=2)
                    tq = (t1 if k % 2 == 0 else t2)[0:R, 256:512].bitcast(BF16)[:, 0:256]
                    ktq = ('tq', k % 2)
                    tqv = tq.rearrange("p (h d) -> p h d", h=2)
                    di = 0 if which == 0 else 1
                    decb = dec_t[0:R, kind, di, hp * 2:hp * 2 + 2].unsqueeze(2).broadcast_to([R, 2, 128])
                    TTo('dve', tqv, rv2, decb, ALU.mult, [('T', 4, 0), 'dec'], [ktq])
                    if which == 1:
                        decb2 = dec_t[0:R, kind, 2, hp * 2:hp * 2 + 2].unsqueeze(2).broadcast_to([R, 2, 128])
                        TTo('pool', K2[0:R, k, :].rearrange("p (h d) -> p h d", h=2), rv2, decb2, ALU.mult,
                            [('T', 4, 0), 'dec'], [('K2', k)])
                    def part_b(k=k, R=R, tq=tq, ktq=ktq, which=which):
                        bt = bank()
                        pbt = ps[bt][:, 0:128].bitcast(BF16)
                        for hh in range(2):
                            TR(pbt[:, hh * 128:hh * 128 + R], tq[:, hh * 128:(hh + 1) * 128], id16[0:R, 0:R], [ktq, 'id16'], [('ps', bt)])
                        dst = QT if which == 0 else KT
                        c0 = TK[k][0]
                        CP('act', dst[:, :, c0:c0 + R], pbt.rearrange("p (h r) -> p h r", h=2)[:, :, 0:R],
                           [('ps', bt)], [('QT' if which == 0 else 'KT', t) for t in tts_of(c0, c0 + R)])
                    pending.append(part_b)
                for k in range(9):
                    tm_proj(wqt, wqk, 256, k, cons_qk)
                while pending:
                    pending.pop(0)()
            for hh in range(2):
                h = hp * 2 + hh
                wvt, wvk = wtile(win_d[l][:, 1024 + h * 256:1024 + (h + 1) * 256], 8)

                def cons_v(k, R, b):
                    CP('act', Vt[0:R, k, :], ps[b][0:R, 0:256], [('ps', b)], [('V', k)])
                for k in range(9):
                    tm_proj(wvt, wvk, 256, k, cons_v)
                gC = gam[h] ** 64
                for pr in range(8):
                    bs = bank()
                    for e_ in range(2):
                        cc0 = (pr * 2 + e_) * 64
                        tts = tts_of(cc0, cc0 + 64)
                        MM(ps[bs][e_ * 64:e_ * 64 + 64, 0:64], KT[:, hh, cc0:cc0 + 64], QT[:, hh, cc0:cc0 + 64], True, True,
                           [('KT', t) for t in tts] + [('QT', t) for t in tts], [('ps', bs)])
                    TTo('dve', scm_all[:, pr, :], ps[bs][:, 0:64], mk_t[:, :], ALU.mult, [('ps', bs), 'mk'], [('scm_all', pr)])

                def sh16(c):
                    return T[c // 8][:, :].bitcast(BF16)[:, (c % 8) * 256:(c % 8) * 256 + 256]
                cur, alt = SR[l][h], SRalt
                kcur, kalt = ('SR', l, h), ('SRalt',)
                CP('act', sh16(0), cur[:], [kcur], [('sh16', 0)] + kall('T', 0))
                for c in range(16):
                    k = c // 2
                    base = (c % 2) * 64
                    bk = bank()
                    MM(ps[bk][:, 0:256], K2[base:base + 64, k, hh * 128:(hh + 1) * 128], Vt[base:base + 64, k, :], True, True,
                       [('K2', k), ('V', k)], [('ps', bk)])
                    (src, ksrc), (dst, kdst) = ((cur, kcur), (alt, kalt)) if c % 2 == 0 else ((alt, kalt), (cur, kcur))
                    STT(dst[:], src[:], gC, ps[bk][:, 0:256], ALU.mult, ALU.add, [ksrc, ('ps', bk)], [kdst])
                    if c < 15:
                        CP('act', sh16(c + 1), dst[:], [kdst], [('sh16', c + 1)] + [('T', (c + 1) // 8, t) for t in range(3)])
                ob_bank = None
                for c in range(16):
                    k = c // 2
                    base = (c % 2) * 64
                    cc0 = c * 64
                    tts = tts_of(cc0, cc0 + 64)
                    if c % 4 == 0:
                        ob_bank = lbank()
                    for vc in range(2):
                        oc = vc * 256 + (c % 4) * 64
                        MM(ps[ob_bank][:, oc:oc + 64], Vt[base:base + 64, k, vc * 128:(vc + 1) * 128], scm_all[base:base + 64, c // 2, :],
                           True, False, [('V', k), ('scm_all', c // 2)], [('ps', ob_bank)])
                        MM(ps[ob_bank][:, oc:oc + 64], sh16(c)[:, vc * 128:(vc + 1) * 128], QT[:, hh, cc0:cc0 + 64],
                           False, True, [('sh16', c)] + [('T', c // 8, t) for t in range(3)] + [('QT', t) for t in tts], [('ps', ob_bank)])
                    if c % 4 == 3:
                        g0 = (c - 3) * 64
                        for vc in range(2):
                            CP('act', OT[vc][:, g0:g0 + 256], ps[ob_bank][:, vc * 256:(vc + 1) * 256], [('ps', ob_bank)],
                               [('T', 5 + vc, t) for t in tts_of(g0, g0 + 256)])
                bs = bank()
                MM(ps[bs][0:32, 0:32], KT[:, hh, NP:N], QT[:, hh, NP:N], True, True, [('KT', 2), ('QT', 2)], [('ps', bs)])
                TTo('dve', scm[0][0:32, 0:32], ps[bs][0:32, 0:32], mks_t[:, :], ALU.mult, [('ps', bs), 'mks'], [('scm', 0)])
                TTo('pool', vbig_r[:, :, :], Vt[0:32, 8, :].unsqueeze(1).broadcast_to([32, 8, 256]),
                    oh_t[:, :].unsqueeze(2).broadcast_to([32, 8, 256]), ALU.mult, [('V', 8), 'oh'], kall('T', 7))
                bo2 = [lbank(), lbank()]
                for vc in range(2):
                    MM(ps[bo2[vc]][:, 0:32], Vt[0:32, 8, vc * 128:(vc + 1) * 128], scm[0][0:32, 0:32], True, False,
                       [('V', 8), ('scm', 0)], [('ps', bo2[vc])])
                g4 = gam[h] ** 4
                SINr = SIN[:, :].rearrange("p (b v) -> p b v", b=4)
                SOUTr = SOUT[:, :].rearrange("p (b v) -> p b v", b=4)
                S16r = SR16S[:, :].rearrange("p (b v) -> p b v", b=4)
                for g_ in range(2):
                    b0 = p * 8 + g_ * 4
                    DMA(SINr, sret[l, b0:b0 + 4, h].rearrange("b d v -> d b v"), (), ['SIN'])
                    CP('pool', S16r, SINr, ['SIN'], ['SR16S'])
                    for j_ in range(4):
                        bq = g_ * 4 + j_
                        for vc in range(2):
                            MM(ps[bo2[vc]][:, bq * 4:bq * 4 + 4], S16r[:, j_, vc * 128:(vc + 1) * 128],
                               QT[:, hh, NP + bq * 4:NP + bq * 4 + 4], False, (bq == NSQ - 1), ['SR16S', ('QT', 2)], [('ps', bo2[vc])])
                        bk = bank()
                        MM(ps[bk][:, 0:256], K2[0:32, 8, hh * 128:(hh + 1) * 128], vbig_r[:, bq, :], True, True, [('K2', 8)] + kall('T', 7), [('ps', bk)])
                        STT(SOUTr[:, j_, :], SINr[:, j_, :], g4, ps[bk][:, 0:256], ALU.mult, ALU.add, ['SIN', ('ps', bk)], [('SOUT', j_)])
                    DMA(rets_o[l, b0:b0 + 4, h].rearrange("b d v -> d b v"), SOUTr, [('SOUT', j_) for j_ in range(4)],
                        [('out', 'rets', l, b0, h)] + [('SOUT', j_) for j_ in range(4)])
                for vc in range(2):
                    CP('act', OT[vc][:, NP:N], ps[bo2[vc]][:, 0:32], [('ps', bo2[vc])], [('T', 5 + vc, 2)])
                if p == 1:
                    DMA(retp_o[l, h], SR[l][h][:], [('SR', l, h)], [('out', 'retp', l, h)])
                wgt, wgk = wtile(win_d[l][:, 2048 + h * 256:2048 + (h + 1) * 256], 8)
                for ti, (c0, c1) in enumerate(TT3):
                    head_norm_cols([(lambda a, b_, vc=vc: OT[vc][:, a:b_], ('T', 5 + vc)) for vc in range(2)], 256, ti, c0, c1)
                for vc in range(2):
                    def cons_g(ti, c0, c1, b, vc=vc):
                        w_ = c1 - c0
                        ta = tmpA[ti % 3]
                        ACT(ta[:, 0:w_], ps[b][:, 0:w_], AF.Silu, [('ps', b)], [('tmpA', ti % 3)])
                        TTo('dve', ta[:, 0:w_], ta[:, 0:w_], rstd[:, c0:c1], ALU.mult, [('tmpA', ti % 3), ('rs', ti)], [('tmpA', ti % 3)])
                        TTo('dve', ob[:, h * 2 + vc, c0:c1], OT[vc][:, c0:c1], ta[:, 0:w_], ALU.mult,
                            [('T', 5 + vc, ti), ('tmpA', ti % 3)], [('o', h * 2 + vc, ti)])
                    fm_proj(wgt, wgk, vc, hT, 'h', 8, cons_g)
        MARK('p%d l%d ret_out' % (p, l))
        emit_branch_out(l, 0, wro_d)

    def emit_hgrn(l, p):
        Qf, Ff, LF, Kf, Bc, E1, E2, OTh = T[0], T[1], T[2], T[3], T[4], T[5], T[6], T[7]
        G32 = gt[:].rearrange("p a b -> p (a b)").bitcast(F32)
        G16 = gt[:].rearrange("p a b -> p (a b)")
        gkeys = [('g', kc_, t_) for kc_ in range(8) for t_ in range(3)]
        fine = [('sh32', c_) for c_ in range(17)] + [('sr16', c_) for c_ in range(16)] + [('xs', c_) for c_ in range(4)] + [('vbh',)]
        FENCE(gkeys, gkeys + fine)

        def sh32(c):
            return G32[:, c * 128:(c + 1) * 128]

        def xs(c):
            return G32[:, 2048 + (c % 4) * 128:2048 + (c % 4) * 128 + 128]

        def sr16(c):
            return G16[:, 5120 + c * 128:5120 + (c + 1) * 128]
        vbh = G16[0:32, 7168:8192].rearrange("p (b v) -> p b v", b=8)

        def prep(h):
            sl_ = h % 2
            er_, eb_, ebr_ = er2[:, sl_, :], eb2[:, sl_, :], ebr2[:, sl_, :]
            ker, keb, kebr = ('er', sl_), ('eb', sl_), ('ebr', sl_)
            kq = [('QT%d' % sl_, t) for t in range(3)]
            kk = [('KT%d' % sl_, t) for t in range(3)]
            wq_, wqk = wtile(win_d[l][:, 3072 + h * 128:3072 + (h + 1) * 128], 8, 128)
            wf_, wfk = wtile(win_d[l][:, 4096 + h * 128:4096 + (h + 1) * 128], 8, 128)

            def cons_q(ti, c0, c1, b):
                ACT(Qf[:, c0:c1], ps[b][:, 0:c1 - c0], AF.Silu, [('ps', b)], [('T', 0, ti)])
            fm_proj(wq_, wqk, 0, hT, 'h', 8, cons_q)

            def cons_f(ti, c0, c1, b):
                ACT(Ff[:, c0:c1], ps[b][:, 0:c1 - c0], AF.Sigmoid, [('ps', b)], [('T', 1, ti)])
            fm_proj(wf_, wfk, 0, hT, 'h', 8, cons_f)
            TS('dve', Ff[:, :], Ff[:, :], omlb_t[:, l, h:h + 1], lb_t[:, l, h:h + 1], ALU.mult, ALU.add,
               kall('T', 1) + ['lb', 'omlb'], kall('T', 1))
            ACT(LF[:, :], Ff[:, :], AF.Ln, kall('T', 1), kall('T', 2))
            ACT(Kf[:, :], Ff[:, :], AF.Identity, kall('T', 1), kall('T', 3), bias=1.0, scale=-1.0)
            S.add('dve', lambda e: e.tensor_tensor_scan(out=Bc[:, :], data0=ones_c[:, 0:1].broadcast_to([128, N]), data1=LF[:, :], initial=0.0,
                                                        op0=ALU.mult, op1=ALU.add),
                  kall('T', 2) + ['ones32'], kall('T', 4))
            Bp = Bc[:, 0:NP].rearrange("p (c t) -> p c t", t=64)
            TTo('dve', LF[:, 0:NP].rearrange("p (c t) -> p c t", t=64), Bp, Bp[:, :, 32:33].broadcast_to([128, 16, 64]),
                ALU.subtract, kall('T', 4), kall('T', 2))
            TTo('dve', LF[:, NP:N], Bc[:, NP:N], Bc[:, NP + 16:NP + 17].broadcast_to([128, 32]), ALU.subtract,
                kall('T', 4) + kall('T', 2), kall('T', 2))
            ACT(E1[:, :], LF[:, :], AF.Exp, kall('T', 2), kall('T', 5))
            ACT(E2[:, :], LF[:, :], AF.Exp, kall('T', 2), kall('T', 6), scale=-1.0)
            TTo('dve', QT[:, sl_, :], Qf[:, :], E1[:, :], ALU.mult, kall('T', 0) + kall('T', 5), kq + [('QT', t) for t in range(3)])
            TTo('dve', KT[:, sl_, :], Kf[:, :], E2[:, :], ALU.mult, kall('T', 3) + kall('T', 6), kk + [('KT', t) for t in range(3)])
            e2p = E2[:, 0:NP].rearrange("p (c t) -> p c t", t=64)[:, :, 0]
            ffp = Ff[:, 0:NP].rearrange("p (c t) -> p c t", t=64)[:, :, 0]
            e1p = E1[:, 0:NP].rearrange("p (c t) -> p c t", t=64)[:, :, 63]
            e2s = E2[:, NP:N].rearrange("p (b t) -> p b t", t=4)[:, :, 0]
            ffs = Ff[:, NP:N].rearrange("p (b t) -> p b t", t=4)[:, :, 0]
            e1s = E1[:, NP:N].rearrange("p (b t) -> p b t", t=4)[:, :, 3]
            TTo('dve', er_[:, 0:16], e2p, ffp, ALU.mult, kall('T', 6) + kall('T', 1) + [ker], [ker])
            TTo('dve', er_[:, 16:24], e2s, ffs, ALU.mult, kall('T', 6) + kall('T', 1) + [ker], [ker])
            CP('dve', ebr_[:, 0:16], e1p, kall('T', 5) + [kebr], [kebr])
            CP('dve', ebr_[:, 16:24], e1s, kall('T', 5) + [kebr], [kebr])
            TTo('dve', eb_[:, :], ebr_[:, :], er_[:, :], ALU.mult, [ker, kebr, keb], [keb])

        def recur(h):
            hp, hh = h // 2, h % 2
            sl_ = h % 2
            er_t, eb_t, ebr_t = er2[:, sl_, :], eb2[:, sl_, :], ebr2[:, sl_, :]
            ker, keb, kebr = ('er', sl_), ('eb', sl_), ('ebr', sl_)
            kq = [('QT%d' % sl_, t) for t in range(3)]
            kk = [('KT%d' % sl_, t) for t in range(3)]
            ko = sl_ * 128
            DMA(SIN[:, :].rearrange("p (b v) -> p b v", b=8), shg[l, p * 8:(p + 1) * 8, h].rearrange("b d v -> d b v"), (), ['SIN'])
            for k in range(9):
                c0, R = TK[k]
                bt = bank()
                pbt = ps[bt][:, 0:64].bitcast(BF16)
                TR(pbt[0:R, 0:128], KT[:, sl_, c0:c0 + R], id16[:, :], kk + ['id16'], [('ps', bt)])
                CP('act', K2[0:R, k, ko:ko + 128], pbt[0:R, 0:128], [('ps', bt)], [('K2h%d' % sl_, k), ('K2', k)])
            if hh == 0:
                wi_, wik = wtile(win_d[l][:, 5120 + hp * 256:5120 + (hp + 1) * 256], 8)

                def cons_v(k, R, b):
                    CP('act', Vt[0:R, k, :], ps[b][0:R, 0:256], [('ps', b)], [('V', k)])
                for k in range(9):
                    tm_proj(wi_, wik, 256, k, cons_v)
            for pr in range(8):
                bs = bank()
                for e_ in range(2):
                    cc0 = (pr * 2 + e_) * 64
                    MM(ps[bs][e_ * 64:e_ * 64 + 64, 0:64], KT[:, sl_, cc0:cc0 + 64], QT[:, sl_, cc0:cc0 + 64], True, True,
                       kk + kq, [('ps', bs)])
                TTo('dve', scm_all[:, pr, :], ps[bs][:, 0:64], mk_t[:, :], ALU.mult, [('ps', bs), 'mk'], [('scm_all', pr)])
            for c in range(16):
                k = c // 2
                base = (c % 2) * 64
                bk = bank()
                MM(ps[bk][:, 0:128], K2[base:base + 64, k, ko:ko + 128], Vt[base:base + 64, k, hh * 128:(hh + 1) * 128], True, True,
                   [('K2h%d' % sl_, k), ('V', k)], [('ps', bk)])
                ACT(xs(c), ps[bk][:, 0:128], AF.Identity, [('ps', bk), kebr], [('xs', c % 4)], scale=ebr_t[:, c:c + 1])
                prev, kprev = (SH[l][h][:], [('SH', l, h)]) if c == 0 else (sh32(c), [('sh32', c)])
                dst, kdst = (SH[l][h][:], [('SH', l, h)]) if c == 15 else (sh32(c + 1), [('sh32', c + 1)])
                TS('pool', sr16(c), prev, er_t[:, c:c + 1], None, ALU.mult, None, kprev + [ker], [('sr16', c)])
                STT(dst, prev, eb_t[:, c:c + 1], xs(c), ALU.mult, ALU.add, kprev + [keb, ('xs', c % 4)], kdst)
            ob_bank = None
            for c in range(16):
                k = c // 2
                base = (c % 2) * 64
                cc0 = c * 64
                if c % 8 == 0:
                    ob_bank = lbank()
                oc = (c % 8) * 64
                MM(ps[ob_bank][:, oc:oc + 64], Vt[base:base + 64, k, hh * 128:(hh + 1) * 128], scm_all[base:base + 64, c // 2, :],
                   True, False, [('V', k), ('scm_all', c // 2)], [('ps', ob_bank)])
                MM(ps[ob_bank][:, oc:oc + 64], sr16(c), QT[:, sl_, cc0:cc0 + 64], False, True,
                   [('sr16', c)] + kq, [('ps', ob_bank)])
                if c % 8 == 7:
                    g0 = (c - 7) * 64
                    CP('act', OTh[:, g0:g0 + 512], ps[ob_bank][:, 0:512], [('ps', ob_bank)], [('T', 7, t) for t in tts_of(g0, g0 + 512)])
            bs = bank()
            MM(ps[bs][0:32, 0:32], KT[:, sl_, NP:N], QT[:, sl_, NP:N], True, True, kk + kq, [('ps', bs)])
            TTo('dve', scm[0][0:32, 0:32], ps[bs][0:32, 0:32], mks_t[:, :], ALU.mult, [('ps', bs), 'mks'], [('scm', 0)])
            TTo('pool', vbh[:, :, :], Vt[0:32, 8, hh * 128:(hh + 1) * 128].unsqueeze(1).broadcast_to([32, 8, 128]),
                oh_t[:, :].unsqueeze(2).broadcast_to([32, 8, 128]), ALU.mult, [('V', 8), 'oh'], [('vbh',)])
            bo = lbank()
            MM(ps[bo][:, 0:32], Vt[0:32, 8, hh * 128:(hh + 1) * 128], scm[0][0:32, 0:32], True, False, [('V', 8), ('scm', 0)], [('ps', bo)])
            SINh = SIN[:, :].rearrange("p (b v) -> p b v", b=8)
            SOUTh = SOUT[:, :].rearrange("p (b v) -> p b v", b=8)
            S16h = SR16S[:, :].rearrange("p (b v) -> p b v", b=8)
            TTo('pool', S16h, SINh, er_t[:, 16:24].unsqueeze(2).broadcast_to([128, 8, 128]), ALU.mult, ['SIN', ker], ['SR16S'])
            TTo('pool', SINh, SINh, eb_t[:, 16:24].unsqueeze(2).broadcast_to([128, 8, 128]), ALU.mult, ['SIN', keb], ['SIN'])
            for bq in range(NSQ):
                MM(ps[bo][:, bq * 4:bq * 4 + 4], S16h[:, bq, :], QT[:, sl_, NP + bq * 4:NP + bq * 4 + 4], False, (bq == NSQ - 1),
                   ['SR16S'] + kq, [('ps', bo)])
                bk = bank()
                MM(ps[bk][:, 0:128], K2[0:32, 8, ko:ko + 128], vbh[:, bq, :], True, True, [('K2h%d' % sl_, 8), ('vbh',)], [('ps', bk)])
                STT(SOUTh[:, bq, :], ps[bk][:, 0:128], ebr_t[:, 16 + bq:17 + bq], SINh[:, bq, :], ALU.mult, ALU.add,
                    [('ps', bk), kebr, 'SIN'], [('SOUT', bq)])
            DMA(hgs_o[l, p * 8:(p + 1) * 8, h].rearrange("b d v -> d b v"), SOUTh, [('SOUT', b_) for b_ in range(8)],
                [('out', 'hgs', l, p, h)] + [('SOUT', b_) for b_ in range(8)])
            CP('act', OTh[:, NP:N], ps[bo][:, 0:32], [('ps', bo)], [('T', 7, 2)])
            if p == 1:
                DMA(hgp_o[l, h], SH[l][h][:], [('SH', l, h)], [('out', 'hgp', l, h)])
            wg_, wgk_ = wtile(win_d[l][:, 6144 + h * 128:6144 + (h + 1) * 128], 8, 128)
            for ti, (c0, c1) in enumerate(TT3):
                head_norm_cols([(lambda a, b_: OTh[:, a:b_], ('T', 7))], 128, ti, c0, c1)

            def cons_g(ti, c0, c1, b, h=h):
                w_ = c1 - c0
                ta = tmpA[ti % 3]
                ACT(ta[:, 0:w_], ps[b][:, 0:w_], AF.Silu, [('ps', b)], [('tmpA', ti % 3)])
                STT(ta[:, 0:w_], ta[:, 0:w_], vecs[:, VI(l, 'hgrn_norm'), h:h + 1], rstd[:, c0:c1], ALU.mult, ALU.mult,
                    [('tmpA', ti % 3), ('rs', ti), 'vecs'], [('tmpA', ti % 3)])
                TTo('dve', ob[:, h, c0:c1], OTh[:, c0:c1], ta[:, 0:w_], ALU.mult,
                    [('T', 7, ti), ('tmpA', ti % 3)], [('o', h, ti)])
            fm_proj(wg_, wgk_, 0, hT, 'h', 8, cons_g)

        slotk = [('QT%d' % s_, t) for s_ in range(2) for t in range(3)] + [('KT%d' % s_, t) for s_ in range(2) for t in range(3)] + \
                [('K2h%d' % s_, k_) for s_ in range(2) for k_ in range(9)]
        wholek = [('QT', t) for t in range(3)] + [('KT', t) for t in range(3)] + [('K2', k_) for k_ in range(9)]
        FENCE(wholek, wholek + slotk)
        prep(0)
        for h in range(8):
            if h + 1 < 8:
                prep(h + 1)
            recur(h)
        FENCE(slotk + fine, wholek + gkeys)
        MARK('p%d l%d hgrn_out' % (p, l))
        emit_branch_out(l, 1, who_d)

    which_layers = dbg.get('layers', 2)
    branches = dbg.get('branches', 'lrh')
    for p in range(2):
        if p == 1:
            DMA(cs_t[:], cs_d[1], ['cs'], ['cs'])
        for k, (c0, R) in enumerate(TK):
            yt = ytok[k % 2]
            DMA(yt[0:R, 0:D], xin[p, c0:c0 + R, :], (), kall('T', k % 2))
            for half in range(2):
                b = bank()
                for q in range(4):
                    kc = half * 4 + q
                    TR(ps[b][:, q * 128:q * 128 + R], yt[0:R, kc * 128:(kc + 1) * 128], id32[0:R, 0:R], kall('T', k % 2) + ['id32'], [('ps', b)])
                CP('dve' if half == 0 else 'act', xT[:, half * 4:(half + 1) * 4, c0:c0 + R],
                   ps[b][:, :].rearrange("p (q r) -> p q r", q=4)[:, :, 0:R], [('ps', b)],
                   [('x', kc, t) for kc in range(half * 4, half * 4 + 4) for t in tts_of(c0, c0 + R)])
        for l in range(which_layers):
            MARK('p%d l%d ffn1' % (p, l))
            emit_norm(VI(l, 'ffn1_norm'))
            emit_ffn(l, 0)
            MARK('p%d l%d mixnorm' % (p, l))
            emit_norm(VI(l, 'mix_norm'))
            if 'l' in branches:
                MARK('p%d l%d lru' % (p, l))
                emit_lru(l, p)
            if 'r' in branches:
                MARK('p%d l%d ret' % (p, l))
                emit_ret(l, p)
            if 'h' in branches:
                MARK('p%d l%d hgrn' % (p, l))
                emit_hgrn(l, p)
            MARK('p%d l%d ffn2' % (p, l))
            emit_norm(VI(l, 'ffn2_norm'))
            emit_ffn(l, 1)
        MARK('p%d final' % p)
        for ti, (c0, c1) in enumerate(TT3):
            b = bank()
            for kc in range(8):
                sq = sqt[kc % 4]
                ACT(sq[:, 0:c1 - c0], xT[:, kc, c0:c1], AF.Square, [('x', kc, ti)], [('sqt', kc % 4)])
                MM(ps[b][:, 0:c1 - c0], ones16[:], sq[:, 0:c1 - c0], kc == 0, kc == 7, [('sqt', kc % 4), 'ones16'], [('ps', b)])
            ACT(rstd[:, c0:c1], ps[b][:, 0:c1 - c0], AF.Ln, [('ps', b)], [('rs', ti)], bias=EPS, scale=1.0 / D)
            ACT(rstd[:, c0:c1], rstd[:, c0:c1], AF.Exp, [('rs', ti)], [('rs', ti)], scale=-0.5)
            for kc in range(8):
                STT(xT[:, kc, c0:c1], xT[:, kc, c0:c1], vecs[:, 2 * NVL, kc:kc + 1], rstd[:, c0:c1], ALU.mult, ALU.mult,
                    [('x', kc, ti), ('rs', ti), 'vecs'], [('x', kc, ti)])
        for k, (c0, R) in enumerate(TK):
            yt = ytok[k % 2]
            for half in range(2):
                b = bank()
                for q in range(4):
                    kc = half * 4 + q
                    TR(ps[b][0:R, q * 128:(q + 1) * 128], xT[:, kc, c0:c0 + R], id32[:, :],
                       [('x', kc, t) for t in tts_of(c0, c0 + R)] + ['id32'], [('ps', b)])
                CP('dve' if half == 0 else 'act', yt[0:R, half * 512:(half + 1) * 512], ps[b][0:R, :], [('ps', b)] + kall('T', k % 2), kall('T', k % 2))
            DMA(y_o[p, c0:c0 + R, :], yt[0:R, 0:D], kall('T', k % 2), [('out', 'y', p, k)] + kall('T', k % 2))

    ops = S.ops
    same_engine_sync = dbg.get('same_engine_sync', True)
    relax_war = dbg.get('relax_war', True)
    dma_ops = [o for o in ops if o.dma]
    for i, o in enumerate(dma_ops):
        if i >= KD:
            o.deps.add(dma_ops[i - KD].idx)
    for o in ops:
        nd = set()
        for d in o.deps:
            po = ops[d]
            if po.eng == 'pe' and o.eng == 'pe':
                continue
            if (not same_engine_sync) and po.eng == o.eng and not po.dma:
                continue
            if relax_war and po.eng == o.eng and not po.dma and not o.dma and d not in o.raw:
                continue
            nd.add(d)
        o.deps = nd
        for d in nd:
            ops[d].ndep += 1
    eng_names = ['pe', 'act', 'dve', 'pool', 'sp']
    sems = {e: es.enter_context(nc.semaphore("s_" + e)) for e in eng_names}
    dsems = [es.enter_context(nc.semaphore("d%d" % i)) for i in range(KD)]
    cnt = {e: 0 for e in eng_names}
    for i, o in enumerate(dma_ops):
        o.sem = dsems[i % KD]
        o.val = 16 * (i // KD + 1)
    for o in ops:
        if o.dma:
            continue
        if o.ndep > 0:
            cnt[o.eng] += 1
            o.sem = sems[o.eng]
            o.val = cnt[o.eng]
    per_eng = {e: [o for o in ops if o.eng == e] for e in eng_names}
    stats = {e: len(per_eng[e]) for e in eng_names}
    stats['marks'] = marks

    def emit_engine(e, ename):
        waited = {}
        for o in per_eng[ename]:
            need = {}
            for d in o.deps:
                po = ops[d]
                key = id(po.sem)
                if po.val > need.get(key, (None, 0))[1]:
                    need[key] = (po.sem, po.val)
            for key, (sem, val) in need.items():
                if waited.get(key, 0) >= val:
                    continue
                e.wait_ge(sem, val)
                waited[key] = val
            ins = o.fn(e)
            if o.dma:
                ins.then_inc(o.sem, 16)
            elif o.sem is not None:
                ins.then_inc(o.sem, 1)
        if ename == 'sp':
            nd = len(dma_ops)
            for j in range(KD):
                n_j = (nd - j + KD - 1) // KD if nd > j else 0
                if n_j > 0:
                    e.wait_ge(dsems[j], 16 * n_j)

    with nc.Block() as block:
        @block.tensor
        def _(e):
            emit_engine(e, 'pe')

        @block.scalar
        def _(e):
            emit_engine(e, 'act')

        @block.vector
        def _(e):
            emit_engine(e, 'dve')

        @block.gpsimd
        def _(e):
            emit_engine(e, 'pool')

        @block.sync
        def _(e):
            emit_engine(e, 'sp')
    es.close()
    return nc, stats


def _tables():
    inv = (np.float32(10000.0) ** (-(np.arange(0, 128, 2, dtype=np.float32)) / np.float32(128))).astype(np.float32)
    cs = np.zeros((2, 128, 9, 2, 64), np.float32)
    for p in range(2):
        for k in range(9):
            if k < 8:
                pos = (p * 1024 + k * 128 + np.arange(128)).astype(np.float32)
            else:
                pos = (PAST_LEN + (np.arange(128) % 4)).astype(np.float32)
            ang = (pos[:, None] * inv[None, :]).astype(np.float32).astype(np.float64)
            cs[p, :, k, 0, :] = np.cos(ang)
            cs[p, :, k, 1, :] = np.sin(ang)
    g = np.array([1.0 - 2.0 ** (-5 - h) for h in range(4)], np.float64)
    dec = np.zeros((128, 2, 3, 4), np.float64)
    sc = 128.0 ** -0.5
    for kind, C in ((0, 64), (1, 4)):
        tl = (np.arange(128) % C).astype(np.float64)[:, None]
        dec[:, kind, 0, :] = g[None, :] ** (tl + 1.0)
        dec[:, kind, 1, :] = g[None, :] ** (-(tl + 1.0)) * sc
        dec[:, kind, 2, :] = g[None, :] ** (C - 1.0 - tl) * sc
    s = np.arange(128)[:, None] % 64
    t = np.arange(64)[None, :]
    mk = (s <= t).astype(np.float32)
    s = np.arange(32)[:, None]
    t = np.arange(32)[None, :]
    mks = ((s // 4 == t // 4) & (s <= t)).astype(np.float32)
    oh = (np.arange(32)[:, None] // 4 == np.arange(8)[None, :]).astype(np.float32)
    ident = np.eye(128, dtype=np.float32)
    return cs, dec.astype(np.float32), mk, mks, oh, ident


_CACHE = {}


def kernel(**inp):
    f = lambda a: np.ascontiguousarray(np.asarray(a, dtype=np.float32))
    dbg = {}
    if os.environ.get('KDBG_LAYERS'):
        dbg['layers'] = int(os.environ['KDBG_LAYERS'])
    if os.environ.get('KDBG_BRANCHES') is not None:
        dbg['branches'] = os.environ['KDBG_BRANCHES']
    if os.environ.get('KDBG_FFNG'):
        dbg['ffn_groups'] = int(os.environ['KDBG_FFNG'])
    if os.environ.get('KDBG_FFNB'):
        dbg['ffn_b'] = int(os.environ['KDBG_FFNB'])
    if os.environ.get('KDBG_CAST'):
        dbg['cast_pat'] = tuple(os.environ['KDBG_CAST'].split(','))
    if os.environ.get('KDBG_KD'):
        dbg['kd'] = int(os.environ['KDBG_KD'])
    if os.environ.get('KDBG_PD'):
        dbg['pd'] = int(os.environ['KDBG_PD'])
    if os.environ.get('KDBG_CASTBIG'):
        dbg['cast_big'] = tuple(os.environ['KDBG_CASTBIG'].split(','))
    if os.environ.get('KDBG_RELAX'):
        dbg['relax_war'] = True
    if os.environ.get('KDBG_NOSES'):
        dbg['same_engine_sync'] = False
    key = tuple(sorted(dbg.items()))
    if key not in _CACHE:
        _CACHE[key] = build_program(dbg)
    nc, stats = _CACHE[key]
    cs, dec, mk, mks, oh, ident = _tables()
    vl = []
    for l in range(2):
        d = {'ffn1_norm': inp['ffn1_norm'][l], 'mix_norm': inp['mix_norm'][l], 'ffn2_norm': inp['ffn2_norm'][l],
             'mb0': inp['merge_bias'][l, 0], 'mb1': inp['merge_bias'][l, 1], 'mb2': inp['merge_bias'][l, 2],
             'lb_logits': inp['hgrn_lb_logits'][l], 'hgrn_norm': inp['hgrn_norm'][l],
             'cw0': inp['conv_w'][l, 0], 'cw1': inp['conv_w'][l, 1], 'cw2': inp['conv_w'][l, 2], 'cw3': inp['conv_w'][l, 3],
             'conv_b': inp['conv_b'][l], 'lru_b_a': inp['lru_b_a'][l], 'lru_b_x': inp['lru_b_x'][l], 'lru_lambda': inp['lru_lambda'][l]}
        for n_ in VN:
            vl.append(np.asarray(d[n_], np.float32))
    vl.append(np.asarray(inp['final_norm'], np.float32))
    vecs = np.ascontiguousarray(np.stack(vl, 0).reshape(NV, 8, 128).transpose(2, 0, 1))
    shared = {k_: f(inp[k_]) for k_ in ['ffn1_w_gate', 'ffn2_w_gate', 'ffn1_w_up', 'ffn2_w_up', 'ffn1_w_down', 'ffn2_w_down',
                                        'w_in', 'w_ret_o', 'w_hgrn_o', 'w_lru_o', 'w_mix_out', 'lru_w_a', 'lru_w_x']}
    shared.update(vecs=vecs, tab_cs=cs, tab_dec=dec, tab_mask=mk, tab_masks=mks, tab_oh=oh, tab_ident=ident)
    xp = f(inp['x_prompt']); xs = f(inp['x_sample'])
    in_maps = []
    for c in range(NCORES):
        xin = np.empty((2, N, D), np.float32)
        for p in range(2):
            xin[p, :NP] = xp[c, p * NP:(p + 1) * NP]
            xin[p, NP:] = xs[c * 16 + p * 8:c * 16 + (p + 1) * 8].reshape(NS, D)
        m = dict(shared)
        m['xin'] = xin
        sl = slice(c * 16, (c + 1) * 16)
        m['sret'] = f(inp['state_ret'][:, sl])
        m['shg'] = f(inp['state_hgrn'][:, sl])
        m['slru'] = f(inp['state_lru'][:, sl])
        m['sconv'] = f(inp['state_conv'][:, sl])
        in_maps.append(m)
    ncr = int(os.environ.get('KDBG_CORES', NCORES))
    if os.environ.get('KDBG_TRACE'):
        res = run_bass_kernel_spmd(nc, in_maps[:ncr], core_ids=list(range(ncr)), trace=True)
        print("EXEC_NS", res.exec_time_ns)
    else:
        res = run_bass_kernel_spmd(nc, in_maps[:ncr], core_ids=list(range(ncr)))
    R = list(res.results)
    while len(R) < NCORES:
        R.append(R[0])
    y_p = np.empty((8, SEQ, D), np.float32)
    y_s = np.empty((128, 4, D), np.float32)
    for c in range(NCORES):
        y = R[c]['y']
        for p in range(2):
            y_p[c, p * NP:(p + 1) * NP] = y[p, :NP]
            y_s[c * 16 + p * 8:c * 16 + (p + 1) * 8] = y[p, NP:].reshape(8, 4, D)
    cat = lambda name, ax: np.ascontiguousarray(np.concatenate([R[c][name] for c in range(NCORES)], axis=ax))
    stk = lambda name: np.ascontiguousarray(np.stack([R[c][name] for c in range(NCORES)], axis=1))
    return (y_p, y_s, stk('retp'), cat('rets', 1), stk('hgp'), cat('hgs', 1),
            stk('lrup'), cat('lrus', 1), stk('convp'), cat('convs', 1))
```
